# Optimizing a Trainium2 kernel written in Bass

```python
import math
import jax, jax.numpy as jnp
from jax import lax
import numpy as np

D_MODEL = 2048
BATCH = 8
SEQ = 2048
DEPTH = 1

CHUNK = 64
N_META = 16
Q_BLOCK = 128
EPS = 1e-6

D_SSM = D_MODEL // 2
SSM_GROUP = 16
N_SSM_GROUPS = D_SSM // SSM_GROUP
SSM_STATE = 64
DT_MIN = 1e-3
DT_MAX = 1e-1

MLA_HEADS = 8
QK_NOPE = 128
QK_ROPE = 64
V_HEAD = 128
Q_LORA = 512
KV_LORA = 256
D_ATTN = MLA_HEADS * V_HEAD
ROPE_BASE = 10000.0

D_MIX = D_SSM + D_ATTN
D_IN = D_SSM + Q_LORA + KV_LORA + QK_ROPE

D_FF = 5504
CONV_W = 3

kernel_name = "hybrid_s5_mla_convffn_block"


def rmsnorm(x, g):
    xf = x.astype(jnp.float32)
    y = xf * lax.rsqrt(jnp.mean(xf * xf, axis=-1, keepdims=True) + EPS)
    return (y * g.astype(jnp.float32)).astype(x.dtype)


def rotary(x, cos, sin):
    x1, x2 = jnp.split(x, 2, axis=-1)
    return jnp.concatenate([x1 * cos - x2 * sin, x2 * cos + x1 * sin], axis=-1)


def s5_mixer(u, lam_re, lam_im, log_dt, b_re, b_im, c_re, c_im, d_skip, w_glu, b_glu):
    bsz, L, _ = u.shape
    f32 = jnp.float32
    uf = u.astype(f32).reshape(bsz, L, N_SSM_GROUPS, SSM_GROUP)
    lam = lax.complex(lam_re.astype(f32), lam_im.astype(f32))
    dt = jnp.exp(log_dt.astype(f32))[:, None]
    lam_bar = jnp.exp(lam * dt)
    b = lax.complex(b_re.astype(f32), b_im.astype(f32))
    b_bar = ((lam_bar - 1.0) / lam)[..., None] * b
    bu = jnp.einsum('blgc,gpc->blgp', uf.astype(jnp.complex64), b_bar)
    a = jnp.broadcast_to(lam_bar, bu.shape)

    def combine(e1, e2):
        a1, s1 = e1
        a2, s2 = e2
        return a1 * a2, a2 * s1 + s2

    _, h = lax.associative_scan(combine, (a, bu), axis=1)
    c = lax.complex(c_re.astype(f32), c_im.astype(f32))
    y = jnp.real(jnp.einsum('blgp,gcp->blgc', h, c))
    y = y + d_skip.astype(f32).reshape(N_SSM_GROUPS, SSM_GROUP) * uf
    y = y.reshape(bsz, L, D_SSM)
    g = jax.nn.gelu(y)
    out = g * jax.nn.sigmoid(g @ w_glu.astype(f32) + b_glu.astype(f32))
    return out.astype(u.dtype)


def mla_mixer(q_a, kv_a, k_pe, q_a_norm, w_q_b, kv_a_norm, w_kv_b, cos, sin, chunk_id):
    bsz, L, _ = q_a.shape
    q = (rmsnorm(q_a, q_a_norm) @ w_q_b).reshape(bsz, L, MLA_HEADS, QK_NOPE + QK_ROPE)
    q_nope, q_pe = q[..., :QK_NOPE], q[..., QK_NOPE:]
    q_pe = rotary(q_pe, cos[:, None, :], sin[:, None, :])
    kv = (rmsnorm(kv_a, kv_a_norm) @ w_kv_b).reshape(bsz, L, MLA_HEADS, QK_NOPE + V_HEAD)
    k_nope, v = kv[..., :QK_NOPE], kv[..., QK_NOPE:]
    k_pe = rotary(k_pe, cos, sin)
    scale = 1.0 / math.sqrt(QK_NOPE + QK_ROPE)

    n_blk = -(-L // Q_BLOCK)
    pad = n_blk * Q_BLOCK - L

    def to_blocks(t):
        t = jnp.pad(t, ((0, 0), (0, pad)) + ((0, 0),) * (t.ndim - 2))
        return jnp.moveaxis(t.reshape(bsz, n_blk, Q_BLOCK, *t.shape[2:]), 1, 0)

    q_cid = jnp.pad(chunk_id, (0, pad), constant_values=2 ** 30).reshape(n_blk, Q_BLOCK)

    def attend(args):
        qn, qp, qc = args
        s = jnp.einsum('bqhd,bkhd->bhqk', qn, k_nope, preferred_element_type=jnp.float32)
        s = s + jnp.einsum('bqhr,bkr->bhqk', qp, k_pe, preferred_element_type=jnp.float32)
        mask = chunk_id[None, :] <= qc[:, None]
        s = jnp.where(mask[None, None], s * scale, jnp.finfo(jnp.float32).min)
        p = jax.nn.softmax(s, axis=-1).astype(v.dtype)
        return jnp.einsum('bhqk,bkhd->bqhd', p, v)

    o = lax.map(attend, (to_blocks(q_nope), to_blocks(q_pe), q_cid))
    o = jnp.moveaxis(o, 0, 1).reshape(bsz, n_blk * Q_BLOCK, D_ATTN)[:, :L]
    return o


def conv_ffn(x, w_up, conv_w, conv_b, w_down):
    L = x.shape[1]
    gate, val = jnp.split(x @ w_up, 2, axis=-1)
    gp = jnp.pad(gate, ((0, 0), (CONV_W - 1, 0), (0, 0)))
    gate = sum(conv_w[k] * gp[:, k:k + L] for k in range(CONV_W)) + conv_b
    return (jax.nn.silu(gate) * val) @ w_down


def setup_inputs(seed: int = 0) -> dict:
    key = jax.random.key(seed)
    ks = jax.random.split(key, 32)
    f32 = jnp.float32
    nrm = lambda k, shape, s: jax.random.normal(k, shape, f32) * s
    gain = lambda k, shape: 1.0 + 0.01 * jax.random.normal(k, shape, f32)
    G, P, C = N_SSM_GROUPS, SSM_STATE, SSM_GROUP
    lam_re = -0.5 + 0.01 * jax.random.normal(ks[4], (DEPTH, G, P), f32)
    lam_im = jnp.pi * jnp.arange(P, dtype=f32)[None, None, :] + 0.01 * jax.random.normal(ks[5], (DEPTH, G, P), f32)
    log_dt = jax.random.uniform(ks[6], (DEPTH, G), f32, math.log(DT_MIN), math.log(DT_MAX))
    return {
        "x": jax.random.normal(ks[0], (BATCH, SEQ, D_MODEL), f32),
        "meta_tokens": nrm(ks[1], (N_META, D_MODEL), 1.0),
        "mix_norm": gain(ks[2], (DEPTH, D_MODEL)),
        "w_in": nrm(ks[3], (DEPTH, D_MODEL, D_IN), D_MODEL ** -0.5),
        "lam_re": lam_re,
        "lam_im": lam_im,
        "log_dt": log_dt,
        "b_re": nrm(ks[7], (DEPTH, G, P, C), (2 * C) ** -0.5),
        "b_im": nrm(ks[8], (DEPTH, G, P, C), (2 * C) ** -0.5),
        "c_re": nrm(ks[9], (DEPTH, G, C, P), (2 * P) ** -0.5),
        "c_im": nrm(ks[10], (DEPTH, G, C, P), (2 * P) ** -0.5),
        "d_skip": nrm(ks[11], (DEPTH, D_SSM), 1.0),
        "w_glu": nrm(ks[12], (DEPTH, D_SSM, D_SSM), D_SSM ** -0.5),
        "b_glu": nrm(ks[13], (DEPTH, D_SSM), 0.01),
        "q_a_norm": gain(ks[14], (DEPTH, Q_LORA)),
        "w_q_b": nrm(ks[15], (DEPTH, Q_LORA, MLA_HEADS * (QK_NOPE + QK_ROPE)), Q_LORA ** -0.5),
        "kv_a_norm": gain(ks[16], (DEPTH, KV_LORA)),
        "w_kv_b": nrm(ks[17], (DEPTH, KV_LORA, MLA_HEADS * (QK_NOPE + V_HEAD)), KV_LORA ** -0.5),
        "out_norm_ssm": gain(ks[18], (DEPTH, D_SSM)),
        "out_norm_attn": gain(ks[19], (DEPTH, D_ATTN)),
        "w_out": nrm(ks[20], (DEPTH, D_MIX, D_MODEL), D_MIX ** -0.5),
        "ffn_norm": gain(ks[21], (DEPTH, D_MODEL)),
        "w_up": nrm(ks[22], (DEPTH, D_MODEL, 2 * D_FF), D_MODEL ** -0.5),
        "conv_w": nrm(ks[23], (DEPTH, CONV_W, D_FF), CONV_W ** -0.5),
        "conv_b": nrm(ks[24], (DEPTH, D_FF), 0.01),
        "w_down": nrm(ks[25], (DEPTH, D_FF, D_MODEL), D_FF ** -0.5),
        "final_norm": gain(ks[26], (D_MODEL,)),
    }


def reference(x, meta_tokens, mix_norm, w_in, lam_re, lam_im, log_dt, b_re, b_im, c_re, c_im,
              d_skip, w_glu, b_glu, q_a_norm, w_q_b, kv_a_norm, w_kv_b, out_norm_ssm,
              out_norm_attn, w_out, ffn_norm, w_up, conv_w, conv_b, w_down, final_norm):
    bsz = x.shape[0]
    meta = jnp.broadcast_to(meta_tokens.astype(x.dtype)[None], (bsz, N_META, D_MODEL))
    h = jnp.concatenate([meta, x], axis=1)
    L = h.shape[1]

    pos = jnp.arange(L, dtype=jnp.int32)
    chunk_id = jnp.where(pos < N_META, 0, 1 + (pos - N_META) // CHUNK)
    inv_freq = 1.0 / (ROPE_BASE ** (jnp.arange(0, QK_ROPE, 2, dtype=jnp.float32) / QK_ROPE))
    ang = pos.astype(jnp.float32)[:, None] * inv_freq[None, :]
    cos = jnp.cos(ang).astype(x.dtype)
    sin = jnp.sin(ang).astype(x.dtype)

    for i in range(DEPTH):
        xn = rmsnorm(h, mix_norm[i])
        z = xn @ w_in[i]
        o1 = D_SSM
        o2 = o1 + Q_LORA
        o3 = o2 + KV_LORA
        u, q_a, kv_a, k_pe = z[..., :o1], z[..., o1:o2], z[..., o2:o3], z[..., o3:]
        ya = s5_mixer(u, lam_re[i], lam_im[i], log_dt[i], b_re[i], b_im[i], c_re[i], c_im[i],
                      d_skip[i], w_glu[i], b_glu[i])
        yb = mla_mixer(q_a, kv_a, k_pe, q_a_norm[i], w_q_b[i], kv_a_norm[i], w_kv_b[i],
                       cos, sin, chunk_id)
        y = jnp.concatenate([rmsnorm(ya, out_norm_ssm[i]), rmsnorm(yb, out_norm_attn[i])], axis=-1)
        h = h + y @ w_out[i]
        h = h + conv_ffn(rmsnorm(h, ffn_norm[i]), w_up[i], conv_w[i], conv_b[i], w_down[i])

    return rmsnorm(h, final_norm)[:, N_META:]
```

```python
import contextlib
import math
import os
import numpy as np
import ml_dtypes
from concourse.bass_utils import run_bass_kernel_spmd

import concourse.bass as bass
import concourse.mybir as mybir

ENGS = ("pe", "dve", "act", "pool", "sp")


class Res:
    __slots__ = ("name", "last_w", "rd_eng", "rd_dma", "excl")

    def __init__(self, name, excl=False):
        self.name = name
        self.excl = excl
        self.last_w = None
        self.rd_eng = {}
        self.rd_dma = []


class Op:
    __slots__ = ("eng", "fn", "waits", "inc", "dma_tok", "idx")


class Prog:
    def __init__(self, nc):
        self.nc = nc
        self.ops = {e: [] for e in ENGS}
        self.known = {e: {} for e in ENGS}
        self.dma_sems = []
        self.nres = 0
        self.stopped = False

    def res(self, name=None):
        self.nres += 1
        return Res(name or f"r{self.nres}")

    def pres(self, name=None):
        self.nres += 1
        return Res(name or f"p{self.nres}", excl=True)

    def new_dma_sem(self, background=False):
        self.dma_sems.append([None, 0, background])
        return len(self.dma_sems) - 1

    def _add(self, eng, fn, reads, writes, dma_sem=None):
        if self.stopped:
            return None
        ex = [r for r in reads if r.excl]
        if ex:
            writes = list(writes) + [r for r in ex if r not in writes]
        op = Op()
        op.eng = eng
        op.fn = fn
        op.inc = False
        op.dma_tok = None
        op.idx = len(self.ops[eng])
        deps = []
        for r in reads:
            if r.last_w is not None:
                deps.append(r.last_w)
        for w in writes:
            if w.last_w is not None:
                if not (dma_sem is not None and w.last_w[0] == "d" and w.last_w[1] == dma_sem):
                    deps.append(w.last_w)
            for e, i in w.rd_eng.items():
                deps.append(("e", e, i))
            deps.extend(w.rd_dma)
        if dma_sem is not None:
            s = self.dma_sems[dma_sem]
            s[1] += 16
            tok = ("d", dma_sem, s[1])
            op.dma_tok = tok
        else:
            tok = ("e", eng, op.idx)
        need = {}
        for d in deps:
            if d[0] == "e":
                if d[1] == eng and eng == "pe":
                    continue
                key = ("e", d[1])
                v = d[2]
            else:
                key = ("d", d[1])
                v = d[2]
            if v > need.get(key, -1):
                need[key] = v
        waits = []
        kn = self.known[eng]
        for key, v in need.items():
            if kn.get(key, -1) >= v:
                continue
            kn[key] = v
            waits.append((key, v))
            if key[0] == "e":
                self.ops[key[1]][v].inc = True
        op.waits = waits
        for r in reads:
            if tok[0] == "e":
                if r.rd_eng.get(eng, -1) < op.idx:
                    r.rd_eng[eng] = op.idx
            else:
                r.rd_dma.append(tok)
        for w in writes:
            w.last_w = tok
            w.rd_eng = {}
            w.rd_dma = []
        self.ops[eng].append(op)
        return op

    def pe(self, fn, reads=(), writes=()):
        return self._add("pe", fn, reads, writes)

    def dve(self, fn, reads=(), writes=()):
        return self._add("dve", fn, reads, writes)

    def act(self, fn, reads=(), writes=()):
        return self._add("act", fn, reads, writes)

    def pool(self, fn, reads=(), writes=()):
        return self._add("pool", fn, reads, writes)

    def dma(self, eng, out, in_, reads=(), writes=(), sem=None, **kw):
        assert sem is not None
        return self._add(eng, lambda e: e.dma_start(out=out, in_=in_, **kw), reads, writes, dma_sem=sem)

    def barrier(self, final=False):
        if self.stopped:
            return
        last = {e: len(self.ops[e]) - 1 for e in ENGS}
        for e in ENGS:
            op = Op()
            op.eng = e
            op.fn = None
            op.inc = False
            op.dma_tok = None
            op.idx = len(self.ops[e])
            waits = []
            kn = self.known[e]
            for f in ENGS:
                if f == e or last[f] < 0:
                    continue
                i = last[f]
                while i >= 0 and (self.ops[f][i].dma_tok is not None or self.ops[f][i].fn is None):
                    i -= 1
                if i < 0:
                    continue
                key = ("e", f)
                if kn.get(key, -1) >= i:
                    continue
                kn[key] = i
                waits.append((key, i))
                self.ops[f][i].inc = True
            for sid, s in enumerate(self.dma_sems):
                key = ("d", sid)
                if s[2] and not final:
                    continue
                if s[1] > 0 and kn.get(key, -1) < s[1]:
                    kn[key] = s[1]
                    waits.append((key, s[1]))
            op.waits = waits
            self.ops[e].append(op)

    def emit(self, stack):
        nc = self.nc
        esem = {e: stack.enter_context(nc.semaphore(f"es_{e}")) for e in ENGS}
        for i, s in enumerate(self.dma_sems):
            s[0] = stack.enter_context(nc.semaphore(f"ds_{i}"))
        val = {}
        for e in ENGS:
            c = 0
            for op in self.ops[e]:
                if op.inc:
                    c += 1
                    val[(e, op.idx)] = c
        if os.environ.get("KDBG_SYNC"):
            for e in ENGS:
                ops = self.ops[e]
                print("ENG", e, "nops", len(ops), "ninc", sum(1 for o in ops if o.inc))
                for o in ops[-6:]:
                    print("   idx", o.idx, "inc", o.inc, "val", val.get((e, o.idx)), "dma", o.dma_tok, "fn", o.fn is not None,
                          "waits", [(k, v, (val.get((k[1], v)) if k[0] == "e" else v)) for k, v in o.waits])
        block = stack.enter_context(nc.Block())
        bm = {"pe": block.tensor, "dve": block.vector, "act": block.scalar,
              "pool": block.gpsimd, "sp": block.sync}

        def run(e):
            def body(eng):
                for op in self.ops[e]:
                    for key, v in op.waits:
                        if key[0] == "e":
                            eng.wait_ge(esem[key[1]], val[(key[1], v)])
                        else:
                            eng.wait_ge(self.dma_sems[key[1]][0], v)
                    if op.fn is None:
                        continue
                    ins = op.fn(eng)
                    if op.dma_tok is not None:
                        ins.then_inc(self.dma_sems[op.dma_tok[1]][0], 16)
                    elif op.inc:
                        ins.then_inc(esem[e], 1)
            return body

        for e in ENGS:
            bm[e](run(e))


F32 = mybir.dt.float32
BF16 = mybir.dt.bfloat16
I32 = mybir.dt.int32
AF = mybir.ActivationFunctionType
ALU = mybir.AluOpType
AX = mybir.AxisListType

D = 2048
SEQ = 2048
NMETA = 16
L = SEQ + NMETA
NT = 17
TS = [128] * 16 + [16]
BANKS = [(0, 512), (512, 1024), (1024, 1536), (1536, 2048), (2048, 2064)]
DSSM = 1024
QL = 512
KVL = 256
DIN = 1856
H = 8
DFF = 5504
NFC = 43
EPS = 1e-6
SCALE = 1.0 / math.sqrt(192.0)
TCH = 64
NCH = 33
GROUPS = [(0, 4), (4, 8), (8, 12), (12, 17)]
TWO_PI = 2.0 * math.pi


def cid(pos):
    return 0 if pos < NMETA else 1 + (pos - NMETA) // 64


def chunk_start(c):
    return 0 if c == 0 else NMETA + 64 * (c - 1)


def host_consts():
    c = {}
    c["ident_bf"] = np.eye(128, dtype=np.float32).astype(ml_dtypes.bfloat16)
    c["ident_f"] = np.eye(128, dtype=np.float32)
    c["ones_bf"] = np.ones((128, 128), dtype=np.float32).astype(ml_dtypes.bfloat16)
    pos = np.arange(L, dtype=np.float32)
    inv_freq = (1.0 / (np.float32(10000.0) ** (np.arange(0, 64, 2, dtype=np.float32) / np.float32(64)))).astype(np.float32)
    ang = (pos[:, None] * inv_freq[None, :]).astype(np.float32)
    cos = np.cos(ang).astype(np.float32).T
    sin = np.sin(ang).astype(np.float32).T
    cosT = np.concatenate([cos, cos, cos, cos], axis=0)
    sinT = np.concatenate([-sin, sin, -sin, sin], axis=0)
    c["cosT"] = np.ascontiguousarray(cosT, dtype=np.float32)
    c["sinT"] = np.ascontiguousarray(sinT, dtype=np.float32)
    seg = np.ones((128, L), dtype=np.float32)
    seg[:, ::TCH] = 0.0
    c["segm"] = seg
    c["iota"] = np.tile(np.arange(TCH, dtype=np.float32)[None, :], (128, 1))
    m = np.zeros((128, NT, 128), dtype=np.float32)
    for kt in range(NT):
        k0 = 128 * kt
        qlo = chunk_start(cid(k0))
        for k in range(TS[kt]):
            ck = cid(k0 + k)
            for j in range(128):
                q = qlo + j
                if q < L and ck <= cid(q):
                    m[k, kt, j] = 1.0
    c["amask"] = ((m - 1.0) * 30000.0).astype(ml_dtypes.bfloat16)
    return c


CONST_SPECS = [("ident_bf", [128, 128], BF16), ("ident_f", [128, 128], F32), ("ones_bf", [128, 128], BF16),
               ("cosT", [128, L], F32), ("sinT", [128, L], F32), ("segm", [128, L], F32),
               ("iota", [128, TCH], F32), ("amask", [128, NT, 128], BF16)]

IN_SPECS = [("x", [SEQ, D]), ("meta_tokens", [NMETA, D]), ("mix_norm", [1, D]), ("w_in", [1, D, DIN]),
            ("lam_re", [1, 64, 64]), ("lam_im", [1, 64, 64]), ("log_dt", [1, 64]),
            ("b_re", [1, 64, 64, 16]), ("b_im", [1, 64, 64, 16]), ("c_re", [1, 64, 16, 64]), ("c_im", [1, 64, 16, 64]),
            ("d_skip", [1, DSSM]), ("w_glu", [1, DSSM, DSSM]), ("b_glu", [1, DSSM]),
            ("q_a_norm", [1, QL]), ("w_q_b", [1, QL, 1536]), ("kv_a_norm", [1, KVL]), ("w_kv_b", [1, KVL, 2048]),
            ("out_norm_ssm", [1, 1024]), ("out_norm_attn", [1, 1024]), ("w_out", [1, D, D]),
            ("ffn_norm", [1, D]), ("w_up", [1, D, 2 * DFF]), ("conv_w", [1, 3, DFF]), ("conv_b", [1, DFF]),
            ("w_down", [1, DFF, D]), ("final_norm", [D])]


class _Stop(Exception):
    pass


def build_nc(stage=99, dbg_shape=None):
    nc = bass.Bass("TRN2", target_bir_lowering=False)
    I = {}
    for name, shape in IN_SPECS:
        I[name] = nc.dram_tensor(name, shape, F32, kind="ExternalInput").ap()
    C = {}
    for name, shape, dt in CONST_SPECS:
        C[name] = nc.dram_tensor("c_" + name, shape, dt, kind="ExternalInput").ap()
    out = nc.dram_tensor("out", [SEQ, D], F32, kind="ExternalOutput").ap()
    h1s = nc.dram_tensor("h1s", [L, D], F32, kind="Internal").ap()
    wup_bf = nc.dram_tensor("wup_bf", [NFC, 128, 16, 256], BF16, kind="Internal").ap()
    wdn_bf = nc.dram_tensor("wdn_bf", [16, 128, NFC, 128], BF16, kind="Internal").ap()
    wglu_bf = nc.dram_tensor("wglu_bf", [128, 8, DSSM], BF16, kind="Internal").ap()
    wo_bf = nc.dram_tensor("wo_bf", [4, 128, 16, 512], BF16, kind="Internal").ap()
    dbg = None
    if dbg_shape is not None:
        dbg = nc.dram_tensor("dbg", dbg_shape, F32, kind="ExternalOutput").ap()

    P = Prog(nc)
    top = contextlib.ExitStack()
    top.__enter__()
    nameid = [0]

    ARENA = 207 * 1024
    arena = top.enter_context(nc.sbuf_tensor("arena", [128, ARENA // 2], BF16))
    free = [[0, ARENA]]
    peak = [0]

    def _alloc(nbytes):
        nbytes = (nbytes + 63) // 64 * 64
        for f in free:
            if f[1] - f[0] >= nbytes:
                off = f[0]
                f[0] += nbytes
                peak[0] = max(peak[0], off + nbytes)
                return off, nbytes
        raise RuntimeError(f"arena OOM need {nbytes} free={free}")

    def _release(off, nbytes):
        free.append([off, off + nbytes])
        free.sort()
        m = []
        for f in free:
            if f[0] == f[1]:
                continue
            if m and m[-1][1] == f[0]:
                m[-1][1] = f[1]
            else:
                m.append(f)
        free[:] = m

    def sb(stack, shape, dt, name=None):
        assert shape[0] == 128
        esz = 2 if dt == BF16 else 4
        n = 1
        for d_ in shape[1:]:
            n *= d_
        off, nb = _alloc(n * esz)
        v = arena[:, off // 2:off // 2 + n * esz // 2]
        if dt != BF16:
            v = v.bitcast(dt)
        if len(shape) == 3:
            v = v.rearrange("p (a b) -> p a b", a=shape[1])
        elif len(shape) == 4:
            v = v.rearrange("p (a b c) -> p a b c", a=shape[1], b=shape[2])
        stack.callback(_release, off, nb)
        if os.environ.get("KDBG_ALLOC"):
            print(f"ALLOC {name} off={off} nb={nb} stopped={P.stopped}")
        return v

    def ps(stack, shape, dt, name=None):
        nameid[0] += 1
        return stack.enter_context(nc.psum_tensor(f"{name or 'p'}_{nameid[0]}", shape, dt))

    def tsc(e, out_, in0, s1, s2, op0, op1=None):
        if op1 is None:
            return e.tensor_scalar(out=out_, in0=in0, scalar1=s1, scalar2=None, op0=op0)
        return e.tensor_scalar(out=out_, in0=in0, scalar1=s1, scalar2=s2, op0=op0, op1=op1)

    def hrows(i):
        if i == 0:
            return [(0, 16, ("meta", 0)), (16, 112, ("x", 0))]
        return [(0, TS[i], ("x", 128 * i - 16))]

    def load_h_tile(dst, i, c0, c1, wres, sem, eng="sp"):
        for (r0, n, (src, s0)) in hrows(i):
            ap = I["meta_tokens"] if src == "meta" else I["x"]
            P.dma(eng, dst[r0:r0 + n, 0:c1 - c0], ap[s0:s0 + n, c0:c1], writes=[wres], sem=sem)

    ident_bf = sb(top, [128, 128], BF16, "identb"); r_const = P.res("const")
    ident_f = sb(top, [128, 128], F32, "identf")
    ones_bf = sb(top, [128, 128], BF16, "ones")
    s_const = P.new_dma_sem()
    P.dma("sp", ident_bf[:], C["ident_bf"], writes=[r_const], sem=s_const)
    P.dma("sp", ident_f[:], C["ident_f"], writes=[r_const], sem=s_const)
    P.dma("sp", ones_bf[:], C["ones_bf"], writes=[r_const], sem=s_const)
    s_out = P.new_dma_sem()
    s_dbg = P.new_dma_sem()
    r_cv = P.res("wconv")
    r_cv2 = P.res("wconv2")

    dbg_col = [0]

    def dump(ap, rres, w, npart=128):
        with contextlib.ExitStack() as ds:
            stg = sb(ds, [128, w], F32, "dbgstg")
            rr = P.res()
            P.dve(lambda e: e.tensor_copy(out=stg[0:npart, :], in_=ap), reads=[rres], writes=[rr])
            c0 = dbg_col[0]
            dbg_col[0] += w
            P.dma("sp", dbg[0:npart, c0:c0 + w], stg[0:npart, :], reads=[rr], sem=s_dbg)
            P.barrier()

    def rstd_ops(t, r_t, n_feat):
        P.dve(lambda e: tsc(e, t, t, 1.0 / n_feat, EPS, ALU.mult, ALU.add), reads=[r_t], writes=[r_t])
        P.act(lambda e: e.activation(out=t, in_=t, func=AF.Sqrt), reads=[r_t], writes=[r_t])
        P.dve(lambda e: e.reciprocal(out=t, in_=t), reads=[r_t], writes=[r_t])

    ph = contextlib.ExitStack()
    xnT = sb(ph, [128, 16, L], BF16, "xnT"); r_xnT = [P.res(f"xnT{i}") for i in range(NT)]
    with contextlib.ExitStack() as p1:
        gmix = sb(p1, [128, D], F32, "gmix"); r_g = P.res()
        s_g = P.new_dma_sem()
        P.dma("sp", gmix[:], I["mix_norm"][0:1, :].broadcast_to([128, D]), writes=[r_g], sem=s_g)
        xt = [sb(p1, [128, D], F32, "xt") for _ in range(2)]; r_xt = [P.res() for _ in range(2)]
        s_xt = [P.new_dma_sem() for _ in range(2)]
        junk = sb(p1, [128, D], BF16, "junk"); r_junk = P.res()
        ssq = sb(p1, [128, NT], F32, "ssq"); r_ssq = [P.res() for _ in range(NT)]
        xnb = [sb(p1, [128, D], BF16, "xnb") for _ in range(2)]; r_xnb = [P.res() for _ in range(2)]
        pT = [ps(p1, [128, 8, 128], BF16, "pT") for _ in range(2)]; r_pT = [P.pres() for _ in range(2)]
        tcount = 0
        for i in range(NT):
            n = TS[i]; s = i % 2
            load_h_tile(xt[s], i, 0, D, r_xt[s], s_xt[s])
            P.act(lambda e, s=s, n=n, i=i: e.activation(out=junk[0:n, :], in_=xt[s][0:n, :], func=AF.Square, accum_out=ssq[0:n, i:i + 1]),
                  reads=[r_xt[s]], writes=[r_junk, r_ssq[i]])
            rstd_ops(ssq[0:n, i:i + 1], r_ssq[i], D)
            P.dve(lambda e, s=s, n=n, i=i: e.scalar_tensor_tensor(out=xnb[s][0:n, :], in0=xt[s][0:n, :], scalar=ssq[0:n, i:i + 1], in1=gmix[0:n, :], op0=ALU.mult, op1=ALU.mult),
                  reads=[r_xt[s], r_ssq[i], r_g], writes=[r_xnb[s]])
            for k4 in range(4):
                pp = tcount % 2; tcount += 1
                for j in range(4):
                    kc = k4 * 4 + j
                    P.pe(lambda e, pp=pp, j=j, kc=kc, s=s, n=n: e.transpose(out=pT[pp][:, j, 0:n], in_=xnb[s][0:n, kc * 128:(kc + 1) * 128], identity=ident_bf[0:n, 0:n]),
                         reads=[r_xnb[s], r_const], writes=[r_pT[pp]])
                P.act(lambda e, pp=pp, k4=k4, i=i, n=n: e.activation(out=xnT[:, k4 * 4:k4 * 4 + 4, 128 * i:128 * i + n], in_=pT[pp][:, 0:4, 0:n], func=AF.Copy),
                      reads=[r_pT[pp]], writes=[r_xnT[i]])
        P.barrier()
        if stage == 1:
            dump(xnT[:, 0, 0:512], r_xnT[0], 512)
            dump(xnT[:, 15, 1552:2064], r_xnT[16], 512)
            P.stopped = True

    mix = contextlib.ExitStack()
    mixA = contextlib.ExitStack()
    uT = sb(mix, [128, 8, L], BF16, "uT"); r_uT = [P.res(f"uT{c}") for c in range(8)]
    qagT = sb(mixA, [128, 4, L], BF16, "qagT"); r_qag = P.res()
    kvagT = sb(mixA, [128, 2, L], BF16, "kvagT"); r_kvag = P.res()
    KrT = sb(mixA, [128, L], BF16, "KrT"); r_KrT = P.res()
    KrTo = sb(mixA, [128, L], BF16, "KrTo")
    mixcs = contextlib.ExitStack()
    cosT = sb(mixcs, [128, L], F32, "cosT"); sinT = sb(mixcs, [128, L], F32, "sinT"); r_cs = P.res()
    s_cs = P.new_dma_sem()
    P.dma("sp", cosT[:], C["cosT"], writes=[r_cs], sem=s_cs)
    P.dma("sp", sinT[:], C["sinT"], writes=[r_cs], sem=s_cs)


    with contextlib.ExitStack() as p2:
        win_v = I["w_in"][0].rearrange("(kc p) f -> p kc f", p=128)
        wt = [sb(p2, [128, 16, 128], BF16, "wt") for _ in range(2)]; r_wt = [P.res() for _ in range(2)]
        s_wt = [P.new_dma_sem() for _ in range(2)]
        pz = [ps(p2, [128, 512], F32, "pz") for _ in range(3)]; r_pz = [P.pres() for _ in range(3)]
        sq_q = sb(p2, [128, 4, L], BF16, "sqq"); r_sqq = P.res()
        sq_kv = sb(p2, [128, 2, L], BF16, "sqkv"); r_sqkv = P.res()
        gq = sb(p2, [128, 4], F32, "gq"); gkv = sb(p2, [128, 2], F32, "gkv"); r_gq = P.res()
        s_gq = P.new_dma_sem()
        P.dma("sp", gq[:], I["q_a_norm"][0].rearrange("(c p) -> p c", p=128), writes=[r_gq], sem=s_gq, allow_slow_non_contiguous=True)
        P.dma("sp", gkv[:], I["kv_a_norm"][0].rearrange("(c p) -> p c", p=128), writes=[r_gq], sem=s_gq, allow_slow_non_contiguous=True)
        tmp1 = sb(p2, [128, 512], F32, "tmp1"); r_tmp1 = P.res()
        tmp2 = sb(p2, [128, 512], F32, "tmp2"); r_tmp2 = P.res()
        zc = 0
        chunks = [("u", c, [(0, 128, c * 128)]) for c in range(8)]
        chunks += [("q", c, [(0, 128, 1024 + c * 128)]) for c in range(4)]
        chunks += [("kv", c, [(0, 128, 1536 + c * 128)]) for c in range(2)]
        chunks += [("kpeA", 0, [(0, 64, 1792), (64, 64, 1792)]),
                   ("kpeB", 0, [(0, 32, 1824), (32, 32, 1792), (64, 32, 1824), (96, 32, 1792)])]
        all_xnT = list(r_xnT)
        pA = {}
        if stage == 20:
            chunks = chunks[0:1]
        if stage == 24:
            chunks = chunks[0:3]
        if stage == 26:
            chunks = chunks[0:2]
        if stage == 25:
            chunks = [chunks[0], chunks[8]]
        if stage == 21:
            chunks = chunks[0:9]
        if stage == 22:
            chunks = chunks[0:14]
        for ci, (kind, idx, pieces) in enumerate(chunks):
            s = ci % 2
            for (d0, ncol, s0) in pieces:
                P.dma("pool", wt[s][:, :, d0:d0 + ncol], win_v[:, :, s0:s0 + ncol], writes=[r_wt[s]], sem=s_wt[s])
            for bi, (b0, b1) in enumerate(BANKS):
                N = b1 - b0
                z = zc % 3; zc += 1
                for kc in range(16):
                    P.pe(lambda e, z=z, s=s, kc=kc, b0=b0, b1=b1, N=N: e.matmul(out=pz[z][:, 0:N], lhsT=wt[s][:, kc, :], rhs=xnT[:, kc, b0:b1], start=(kc == 0), stop=(kc == 15)),
                         reads=[r_wt[s]] + (all_xnT if kc == 0 else []), writes=[r_pz[z]])
                if kind == "u":
                    P.act(lambda e, z=z, idx=idx, b0=b0, b1=b1, N=N: e.activation(out=uT[:, idx, b0:b1], in_=pz[z][:, 0:N], func=AF.Copy),
                          reads=[r_pz[z]], writes=[r_uT[idx]])
                elif kind in ("q", "kv"):
                    dstT, sqT, gg, rr, rsq = (qagT, sq_q, gq, r_qag, r_sqq) if kind == "q" else (kvagT, sq_kv, gkv, r_kvag, r_sqkv)
                    P.dve(lambda e, z=z, idx=idx, b0=b0, b1=b1, N=N, dstT=dstT, gg=gg: tsc(e, dstT[:, idx, b0:b1], pz[z][:, 0:N], gg[:, idx:idx + 1], None, ALU.mult),
                          reads=[r_pz[z], r_gq], writes=[rr])
                    P.act(lambda e, z=z, idx=idx, b0=b0, b1=b1, N=N, sqT=sqT: e.activation(out=sqT[:, idx, b0:b1], in_=pz[z][:, 0:N], func=AF.Square),
                          reads=[r_pz[z]], writes=[rsq])
                elif kind == "kpeA":
                    pA[bi] = z
                    P.dve(lambda e, z=z, b0=b0, b1=b1, N=N: e.tensor_tensor(out=KrT[:, b0:b1], in0=pz[z][:, 0:N], in1=cosT[:, b0:b1], op=ALU.mult),
                          reads=[r_pz[z], r_cs], writes=[r_KrT])
                else:
                    P.dve(lambda e, z=z, b0=b0, b1=b1, N=N: e.tensor_tensor(out=tmp1[:, 0:N], in0=pz[z][:, 0:N], in1=sinT[:, b0:b1], op=ALU.mult),
                          reads=[r_pz[z], r_cs], writes=[r_tmp1])
                    P.dve(lambda e, b0=b0, b1=b1, N=N: e.tensor_tensor(out=KrT[:, b0:b1], in0=KrT[:, b0:b1], in1=tmp1[:, 0:N], op=ALU.add),
                          reads=[r_tmp1, r_KrT], writes=[r_KrT])
        if stage not in (20, 21, 24, 25, 26):
            P.pool(lambda e: e.memset(KrTo[0:64, :], 0.0), writes=[r_KrT])
            P.act(lambda e: e.activation(out=KrTo[64:128, :], in_=KrT[64:128, :], func=AF.Copy), reads=[r_KrT], writes=[r_KrT])
            P.pool(lambda e: e.memset(KrT[64:128, :], 0.0), reads=[r_KrT], writes=[r_KrT])
        P.barrier()
        if stage in (20, 21, 22, 23, 24, 25, 26):
            dump(uT[:, 0, 0:512], r_uT[0], 512)
            if stage in (21, 22, 23, 25):
                dump(qagT[:, 0, 0:512], r_qag, 512)
            if stage == 23:
                dump(KrT[:, 0:512], r_KrT, 512)
            P.stopped = True
        ph.close()
        rstd_q = sb(mixA, [128, L], F32, "rstdq"); r_rq = P.res()
        rstd_kv = sb(mixA, [128, L], F32, "rstdkv"); r_rkv = P.res()
        rkv_tok = sb(mixA, [128, NT], F32, "rkvtok"); r_rkt = P.res()
        for (sqT, nk, dst, rdst, rsq, nf) in ((sq_q, 4, rstd_q, r_rq, r_sqq, QL), (sq_kv, 2, rstd_kv, r_rkv, r_sqkv, KVL)):
            for (b0, b1) in BANKS:
                N = b1 - b0
                z = zc % 3; zc += 1
                for kc in range(nk):
                    P.pe(lambda e, z=z, kc=kc, b0=b0, b1=b1, N=N, sqT=sqT, nk=nk: e.matmul(out=pz[z][:, 0:N], lhsT=ones_bf[:], rhs=sqT[:, kc, b0:b1], start=(kc == 0), stop=(kc == nk - 1)),
                         reads=[rsq, r_const], writes=[r_pz[z]])
                P.act(lambda e, z=z, b0=b0, b1=b1, N=N, dst=dst: e.activation(out=dst[:, b0:b1], in_=pz[z][:, 0:N], func=AF.Copy),
                      reads=[r_pz[z]], writes=[rdst])
            rstd_ops(dst[:], rdst, nf)
        P.pool(lambda e: e.memset(rkv_tok[:], 1.0), writes=[r_rkt])
        z = zc % 3; zc += 1
        for i in range(NT):
            n = TS[i]
            for kc in range(2):
                P.pe(lambda e, z=z, i=i, n=n, kc=kc: e.matmul(out=pz[z][0:n, i:i + 1], lhsT=sq_kv[:, kc, 128 * i:128 * i + n], rhs=ones_bf[:, 0:1], start=(kc == 0), stop=(kc == 1)),
                     reads=[r_sqkv, r_const], writes=[r_pz[z]])
            P.dve(lambda e, z=z, i=i, n=n: e.tensor_copy(out=rkv_tok[0:n, i:i + 1], in_=pz[z][0:n, i:i + 1]), reads=[r_pz[z]], writes=[r_rkt])
        rstd_ops(rkv_tok[:], r_rkt, KVL)
        P.barrier()
        if stage == 2:
            dump(uT[:, 0, 0:512], r_uT[0], 512)
            dump(uT[:, 7, 1552:2064], r_uT[7], 512)
            dump(qagT[:, 0, 0:512], r_qag, 512)
            dump(kvagT[:, 1, 0:512], r_kvag, 512)
            dump(KrT[:, 0:512], r_KrT, 512)
            dump(rstd_q[:, 0:512], r_rq, 512)
            dump(rstd_kv[:, 1552:2064], r_rkv, 512)
            dump(rkv_tok[:], r_rkt, NT)
            P.stopped = True

    ybgT = sb(mix, [128, 8, L], BF16, "ybgT"); r_ybg = P.res()
    ssb = sb(mix, [128, NT, 8], F32, "ssb"); r_ssb = P.res()
    P.pool(lambda e: e.memset(ssb[:], 1.0), writes=[r_ssb])
    with contextlib.ExitStack() as p3:
        wq = sb(p3, [128, 4, 1536], BF16, "wq"); r_wq = P.res(); s_wq = P.new_dma_sem()
        wqA = sb(p3, [128, 4, 4, 128], BF16, "wqA"); wqB = sb(p3, [128, 4, 4, 128], BF16, "wqB")
        wkv = sb(p3, [128, 2, 2048], BF16, "wkv")
        wq_v = I["w_q_b"][0].rearrange("(kc p) f -> p kc f", p=128)
        P.dma("pool", wq[:], wq_v, writes=[r_wq], sem=s_wq)
        P.dma("pool", wkv[:], I["w_kv_b"][0].rearrange("(kc p) f -> p kc f", p=128), writes=[r_wq], sem=s_wq)
        for h in range(H):
            c0 = h * 192 + 128
            o = 64 * (h % 2)
            P.dma("pool", wqA[:, :, h // 2, o:o + 64], wq_v[:, :, c0:c0 + 64], writes=[r_wq], sem=s_wq)
            P.dma("pool", wqB[:, :, h // 2, o:o + 32], wq_v[:, :, c0 + 32:c0 + 64], writes=[r_wq], sem=s_wq)
            P.dma("pool", wqB[:, :, h // 2, o + 32:o + 64], wq_v[:, :, c0:c0 + 32], writes=[r_wq], sem=s_wq)
        gb = sb(p3, [128, 8], F32, "gb"); r_gb = P.res(); s_gb = P.new_dma_sem()
        P.dma("sp", gb[:], I["out_norm_attn"][0].rearrange("(c p) -> p c", p=128), writes=[r_gb], sem=s_gb, allow_slow_non_contiguous=True)
        QrT = sb(p3, [128, 4, L], BF16, "QrT"); r_QrT = P.res()
        t1 = sb(p3, [128, 512], F32, "t1"); r_t1 = P.res()
        t2 = sb(p3, [128, 512], F32, "t2"); r_t2 = P.res()
        mq = sb(p3, [128, 8], F32, "mq"); mk = sb(p3, [128, 8], F32, "mk"); negM = sb(p3, [128, 8], F32, "negM"); r_m = P.res()
        mtmp = sb(p3, [128, 1], F32, "mtmp"); r_mtmp = P.res()
        pq = ps(p3, [128, 512], F32, "pq"); r_pq = P.pres()
        pS = [ps(p3, [128, 512], F32, "pS") for _ in range(3)]; r_pS = [P.pres() for _ in range(3)]
        pO = [ps(p3, [128, 512], F32, "pO") for _ in range(2)]; r_pO = [P.pres() for _ in range(2)]
        pD = [ps(p3, [128, 512], F32, "pD") for _ in range(2)]; r_pD = [P.pres() for _ in range(2)]
        P.pool(lambda e: e.memset(mq[:], 0.0), writes=[r_m])
        P.pool(lambda e: e.memset(mk[:], 0.0), writes=[r_m])
        ac = 0
        for j in range(4):
            for (b0, b1) in BANKS:
                N = b1 - b0
                for (wsrc, z) in ((wqA, 0), (wqB, 1)):
                    for kc in range(4):
                        P.pe(lambda e, z=z, wsrc=wsrc, kc=kc, j=j, b0=b0, b1=b1, N=N: e.matmul(out=pS[z][:, 0:N], lhsT=wsrc[:, kc, j, :], rhs=qagT[:, kc, b0:b1], start=(kc == 0), stop=(kc == 3)),
                             reads=[r_wq, r_qag], writes=[r_pS[z]])
                P.dve(lambda e, b0=b0, b1=b1, N=N: e.tensor_tensor(out=t1[:, 0:N], in0=pS[0][:, 0:N], in1=cosT[:, b0:b1], op=ALU.mult), reads=[r_pS[0], r_cs], writes=[r_t1])
                P.dve(lambda e, b0=b0, b1=b1, N=N: e.tensor_tensor(out=t2[:, 0:N], in0=pS[1][:, 0:N], in1=sinT[:, b0:b1], op=ALU.mult), reads=[r_pS[1], r_cs], writes=[r_t2])
                P.dve(lambda e, N=N: e.tensor_tensor(out=t1[:, 0:N], in0=t1[:, 0:N], in1=t2[:, 0:N], op=ALU.add), reads=[r_t1, r_t2], writes=[r_t1])
                P.dve(lambda e, j=j, b0=b0, b1=b1, N=N: e.scalar_tensor_tensor(out=QrT[:, j, b0:b1], in0=t1[:, 0:N], scalar=SCALE, in1=rstd_q[:, b0:b1], op0=ALU.mult, op1=ALU.mult),
                      reads=[r_t1, r_rq], writes=[r_QrT])
        P.barrier()
        mixcs.close()
        s_cv = P.new_dma_sem(background=True)
        wupc_v = I["w_up"][0].rearrange("(kc p) f -> p kc f", p=128)
        wdnc_v = I["w_down"][0].rearrange("(fc p) o -> p fc o", p=128)
        s_cv2 = P.new_dma_sem(background=True)
        P.dma("pool", wglu_bf, I["w_glu"][0].rearrange("(kc p) f -> p kc f", p=128), writes=[r_cv2], sem=s_cv2)
        woc_v = I["w_out"][0].rearrange("(kc p) f -> p kc f", p=128)
        for cb in range(4):
            P.dma("pool", wo_bf[cb], woc_v[:, :, cb * 512:(cb + 1) * 512], writes=[r_cv2], sem=s_cv2)
        for fc in range(NFC):
            P.dma("pool", wup_bf[fc, :, :, 0:128], wupc_v[:, :, fc * 128:(fc + 1) * 128], writes=[r_cv], sem=s_cv)
            P.dma("pool", wup_bf[fc, :, :, 128:256], wupc_v[:, :, DFF + fc * 128:DFF + (fc + 1) * 128], writes=[r_cv], sem=s_cv)
        for oc in range(16):
            P.dma("pool", wdn_bf[oc], wdnc_v[:, :, oc * 128:(oc + 1) * 128], writes=[r_cv], sem=s_cv)
        amask = sb(p3, [128, NT, 128], BF16, "amask"); r_am = P.res()
        P.dma("sp", amask[:], C["amask"], writes=[r_am], sem=s_gb)
        QnT = [sb(p3, [128, L], BF16, "QnT") for _ in range(2)]; r_QnT = [P.res() for _ in range(2)]
        KnT = [sb(p3, [128, L], BF16, "KnT") for _ in range(2)]; r_KnT = [P.res() for _ in range(2)]
        Vh = [sb(p3, [128, NT, 128], BF16, "Vh") for _ in range(2)]; r_Vh = [P.res() for _ in range(2)]
        PT = [sb(p3, [128, 512], BF16, "PT") for _ in range(3)]; r_PT = [P.res() for _ in range(3)]
        sqt = sb(p3, [128, 512], BF16, "sqt"); r_sqt = P.res()
        sqr = sb(p3, [128, 512], BF16, "sqr"); r_sqr = P.res()
        from collections import deque
        acl = [0]
        obc = [0]
        pending = deque()
        prio = deque()
        sqp = sb(p3, [128, 512], BF16, "sqp"); r_sqp = P.res()
        sqrp2 = [sb(p3, [128, 512], BF16, "sqrp") for _ in range(2)]; r_sqrp = P.res()
        P.dve(lambda e: e.memset(sqrp2[0][:], 0.0), writes=[r_sqrp])
        P.dve(lambda e: e.memset(sqrp2[1][:], 0.0), writes=[r_sqrp])

        def proj_micro(h):
            s = h % 2
            ro = 64 * (h % 2)
            mp = []
            for (b0, b1) in BANKS:
                N = b1 - b0

                def g_q(b0=b0, b1=b1, N=N):
                    for kc in range(4):
                        P.pe(lambda e, kc=kc: e.matmul(out=pq[:, 0:N], lhsT=wq[:, kc, h * 192:h * 192 + 128], rhs=qagT[:, kc, b0:b1], start=(kc == 0), stop=(kc == 3)),
                             reads=[r_wq, r_qag], writes=[r_pq])
                    P.dve(lambda e: e.scalar_tensor_tensor(out=QnT[s][:, b0:b1], in0=pq[:, 0:N], scalar=SCALE, in1=rstd_q[:, b0:b1], op0=ALU.mult, op1=ALU.mult),
                          reads=[r_pq, r_rq], writes=[r_QnT[s]])

                def g_nq0(b0=b0, b1=b1, N=N):
                    P.act(lambda e: e.activation(out=sqp[:, 0:N], in_=QnT[s][:, b0:b1], func=AF.Square), reads=[r_QnT[s]], writes=[r_sqp])
                    P.act(lambda e: e.activation(out=sqrp2[h % 2][ro:ro + 64, 0:N], in_=QrT[ro:ro + 64, h // 2, b0:b1], func=AF.Square), reads=[r_QrT], writes=[r_sqrp])

                def g_nq(b0=b0, b1=b1, N=N):
                    P.pe(lambda e: e.matmul(out=pq[:, 0:N], lhsT=ones_bf[:], rhs=sqp[:, 0:N], start=True, stop=False), reads=[r_sqp, r_const], writes=[r_pq])
                    P.pe(lambda e: e.matmul(out=pq[:, 0:N], lhsT=ones_bf[:], rhs=sqrp2[h % 2][:, 0:N], start=False, stop=True), reads=[r_sqrp, r_const], writes=[r_pq])
                    P.dve(lambda e: e.tensor_reduce(out=mtmp[:], in_=pq[:, 0:N], axis=AX.X, op=ALU.max), reads=[r_pq], writes=[r_mtmp])
                    P.dve(lambda e: e.tensor_tensor(out=mq[:, h:h + 1], in0=mq[:, h:h + 1], in1=mtmp[:], op=ALU.max), reads=[r_mtmp, r_m], writes=[r_m])

                def g_k(b0=b0, b1=b1, N=N):
                    for kc in range(2):
                        P.pe(lambda e, kc=kc: e.matmul(out=pq[:, 0:N], lhsT=wkv[:, kc, h * 256:h * 256 + 128], rhs=kvagT[:, kc, b0:b1], start=(kc == 0), stop=(kc == 1)),
                             reads=[r_wq, r_kvag], writes=[r_pq])
                    P.dve(lambda e: e.tensor_tensor(out=KnT[s][:, b0:b1], in0=pq[:, 0:N], in1=rstd_kv[:, b0:b1], op=ALU.mult),
                          reads=[r_pq, r_rkv], writes=[r_KnT[s]])

                def g_nk0(b0=b0, b1=b1, N=N):
                    P.act(lambda e: e.activation(out=sqp[:, 0:N], in_=KnT[s][:, b0:b1], func=AF.Square), reads=[r_KnT[s]], writes=[r_sqp])
                    P.act(lambda e: e.activation(out=sqrp2[0][0:64, 0:N], in_=KrT[0:64, b0:b1], func=AF.Square), reads=[r_KrT], writes=[r_sqrp])

                def g_nk(b0=b0, b1=b1, N=N):
                    P.pe(lambda e: e.matmul(out=pq[:, 0:N], lhsT=ones_bf[:], rhs=sqp[:, 0:N], start=True, stop=False), reads=[r_sqp, r_const], writes=[r_pq])
                    P.pe(lambda e: e.matmul(out=pq[:, 0:N], lhsT=ones_bf[:], rhs=sqrp2[0][:, 0:N], start=False, stop=True), reads=[r_sqrp, r_const], writes=[r_pq])
                    P.dve(lambda e: e.tensor_reduce(out=mtmp[:], in_=pq[:, 0:N], axis=AX.X, op=ALU.max), reads=[r_pq], writes=[r_mtmp])
                    P.dve(lambda e: e.tensor_tensor(out=mk[:, h:h + 1], in0=mk[:, h:h + 1], in1=mtmp[:], op=ALU.max), reads=[r_mtmp, r_m], writes=[r_m])
                mp += [g_q, g_nq0, g_k, g_nq, g_nk0, g_nk]
            for i in range(NT):
                def g_v(i=i, n=TS[i]):
                    for kc in range(2):
                        P.pe(lambda e, kc=kc: e.matmul(out=pq[0:n, 0:128], lhsT=kvagT[:, kc, 128 * i:128 * i + n], rhs=wkv[:, kc, h * 256 + 128:h * 256 + 256], start=(kc == 0), stop=(kc == 1)),
                             reads=[r_wq, r_kvag], writes=[r_pq])
                    P.dve(lambda e: tsc(e, Vh[s][0:n, i, :], pq[0:n, 0:128], rkv_tok[0:n, i:i + 1], None, ALU.mult), reads=[r_pq, r_rkt], writes=[r_Vh[s]])
                mp.append(g_v)

            def g_m():
                P.dve(lambda e: e.tensor_tensor(out=negM[:, h:h + 1], in0=mq[:, h:h + 1], in1=mk[:, h:h + 1], op=ALU.mult), reads=[r_m], writes=[r_m])
                P.act(lambda e: e.activation(out=negM[:, h:h + 1], in_=negM[:, h:h + 1], func=AF.Sqrt), reads=[r_m], writes=[r_m])
                P.dve(lambda e: tsc(e, negM[:, h:h + 1], negM[:, h:h + 1], -1.0, None, ALU.mult), reads=[r_m], writes=[r_m])
            mp.append(g_m)
            return mp

        def attn_bank(h, b0, b1):
            s = h % 2
            ro = 64 * (h % 2)
            ob = obc[0] % 2; obc[0] += 1
            kts = [kt for kt in range(NT) if chunk_start(cid(128 * kt)) < b1]
            infos = []
            for kt in kts:
                nk = TS[kt]
                k0 = 128 * kt
                qlo = chunk_start(cid(k0))
                qfull = chunk_start(cid(min(k0 + 127, L - 1)))
                qs = max(qlo, b0); qe = b1
                a = acl[0] % 3; acl[0] += 1
                infos.append((kt, nk, k0, qlo, qfull, qs, qe, qe - qs, a))

            def qk(info):
                kt, nk, k0, qlo, qfull, qs, qe, W, a = info
                P.pe(lambda e: e.matmul(out=pS[a][0:nk, 0:W], lhsT=KnT[s][:, k0:k0 + nk], rhs=QnT[s][:, qs:qe], start=True, stop=False),
                     reads=[r_KnT[s], r_QnT[s]], writes=[r_pS[a]])
                pe_ = min(qfull, qe)
                part = pe_ > qs
                P.pe(lambda e: e.matmul(out=pS[a][0:nk, 0:W], lhsT=(KrT if ro == 0 else KrTo)[:, k0:k0 + nk], rhs=QrT[:, h // 2, qs:qe], start=False, stop=(not part)),
                     reads=[r_KrT, r_QrT], writes=[r_pS[a]])
                if part:
                    Wm = pe_ - qs; j0 = qs - qlo
                    P.pe(lambda e: e.matmul(out=pS[a][0:nk, 0:Wm], lhsT=ident_bf[0:nk, 0:nk], rhs=amask[0:nk, kt, j0:j0 + Wm], start=False, stop=True),
                         reads=[r_const, r_am], writes=[r_pS[a]])
                P.act(lambda e: e.activation(out=PT[a][0:nk, 0:W], in_=pS[a][0:nk, 0:W], func=AF.Exp, bias=negM[0:nk, h:h + 1], scale=1.0),
                      reads=[r_pS[a], r_m], writes=[r_PT[a]])

            def pvd(info, first, last):
                kt, nk, k0, qlo, qfull, qs, qe, W, a = info
                P.pe(lambda e: e.matmul(out=pO[ob][:, qs - b0:qs - b0 + W], lhsT=Vh[s][0:nk, kt, :], rhs=PT[a][0:nk, 0:W], start=first, stop=last),
                     reads=[r_Vh[s], r_PT[a]], writes=[r_pO[ob]])
                P.pe(lambda e: e.matmul(out=pD[ob][:, qs - b0:qs - b0 + W], lhsT=ones_bf[0:nk, :], rhs=PT[a][0:nk, 0:W], start=first, stop=last),
                     reads=[r_const, r_PT[a]], writes=[r_pD[ob]])

            DEPTH = 2
            for ii in range(len(infos) + DEPTH):
                if ii < len(infos):
                    qk(infos[ii])
                jj = ii - DEPTH
                if jj >= 0:
                    pvd(infos[jj], jj == 0, jj == len(infos) - 1)
                if prio:
                    prio.popleft()()
                if pending:
                    pending.popleft()()
            N = b1 - b0
            epi = []
            for c_ in range(0, N, 128):
                ce = min(c_ + 128, N)
                epi.append(lambda c_=c_, ce=ce: P.dve(lambda e: e.reciprocal(out=t2[:, c_:ce], in_=pD[ob][:, c_:ce]), reads=[r_pD[ob]], writes=[r_t2]))
            epi.append(lambda: P.dve(lambda e: e.tensor_tensor(out=t1[:, 0:N], in0=pO[ob][:, 0:N], in1=t2[:, 0:N], op=ALU.mult), reads=[r_pO[ob], r_t2], writes=[r_t1]))

            def epi_fin():
                P.dve(lambda e: tsc(e, ybgT[:, h, b0:b1], t1[:, 0:N], gb[:, h:h + 1], None, ALU.mult), reads=[r_t1, r_gb], writes=[r_ybg])
                P.act(lambda e: e.activation(out=sqt[:, 0:N], in_=t1[:, 0:N], func=AF.Square), reads=[r_t1], writes=[r_sqt])
            epi.append(epi_fin)
            stats = []
            for i in range(NT):
                if not (b0 <= 128 * i < b1):
                    continue

                def g_s(i=i, n=TS[i]):
                    P.pe(lambda e: e.matmul(out=pq[0:n, 256 + i:257 + i], lhsT=sqt[:, 128 * i - b0:128 * i - b0 + n], rhs=ones_bf[:, 0:1], start=True, stop=True),
                         reads=[r_sqt, r_const], writes=[r_pq])
                    P.dve(lambda e: e.tensor_copy(out=ssb[0:n, i, h:h + 1], in_=pq[0:n, 256 + i:257 + i]), reads=[r_pq], writes=[r_ssb])
                stats.append(g_s)
            prio.extend(epi)
            prio.extend([lambda: None, lambda: None])
            prio.extend(stats)

        for g_ in proj_micro(0):
            g_()
        for h in range(H):
            if h + 1 < H:
                pending.extend(proj_micro(h + 1))
            for (b0, b1) in BANKS:
                attn_bank(h, b0, b1)
            while prio or pending:
                if prio:
                    prio.popleft()()
                if pending:
                    pending.popleft()()
        P.barrier()
        if stage == 3:
            dump(ybgT[:, 0, 0:512], r_ybg, 512)
            dump(ybgT[:, 7, 1552:2064], r_ybg, 512)
            dump(ybgT[:, 3, 512:1024], r_ybg, 512)
            dump(ssb[:].rearrange("p a b -> p (a b)"), r_ssb, NT * 8)
            P.stopped = True

    mixA.close()

    with contextlib.ExitStack() as p4:
        NP = 32
        p4a = contextlib.ExitStack()
        pst = ps(p4, [128, 512], F32, "pst"); r_pst = P.pres()
        pY = [ps(p4, [128, 512], F32, "pY") for _ in range(5)]; r_pY = [P.pres() for _ in range(5)]
        pB = [ps(p4, [128, 512], F32, "pB") for _ in range(2)]; r_pB = [P.pres() for _ in range(2)]
        Fre = sb(p4a, [128, NP, TCH], F32, "Fre"); Fim = sb(p4a, [128, NP, TCH], F32, "Fim")
        Pre = sb(p4a, [128, NP, TCH], F32, "Pre"); Pim = sb(p4a, [128, NP, TCH], F32, "Pim"); r_tab = P.res()
        Bt = sb(p4a, [128, NP, 2, 128], BF16, "Bt"); Ct = sb(p4a, [128, NP, 2, 128], BF16, "Ct"); r_BC = P.res()
        s_bc = P.new_dma_sem()
        P.dve(lambda e: e.memset(Bt[:], 0.0), writes=[r_BC])
        P.dve(lambda e: e.memset(Ct[:], 0.0), writes=[r_BC])
        pxs = contextlib.ExitStack()
        XB = [sb(pxs, [128, 128], F32, "XB") for _ in range(8)]; r_XB = [P.res() for _ in range(8)]; s_XB = [P.new_dma_sem() for _ in range(8)]
        XC = [sb(pxs, [128, 128], F32, "XC") for _ in range(8)]; r_XC = [P.res() for _ in range(8)]; s_XC = [P.new_dma_sem() for _ in range(8)]
        for t_ in XB + XC:
            P.dve(lambda e, t_=t_: e.memset(t_[:], 0.0), writes=r_XB + r_XC)
        sc_ = 0
        for k8 in range(8):
            for ri, (bsrc, csrc) in enumerate(((I["b_re"], I["c_re"]), (I["b_im"], I["c_im"]))):
                sx = sc_ % 8; sc_ += 1
                for par in range(2):
                    src = bsrc[0, 8 * k8 + par:8 * k8 + 8:2].rearrange("j p c -> p j c")
                    dst = XB[sx][64 * par:64 * par + 64, :].rearrange("p (j c) -> p j c", c=16)[:, par::2, :]
                    P.dma("sp", dst, src, writes=[r_XB[sx]], sem=s_XB[sx])
                for j in range(8):
                    P.dma("sp", XC[sx][16 * j:16 * j + 16, 64 * (j % 2):64 * (j % 2) + 64], csrc[0, 8 * k8 + j], writes=[r_XC[sx]], sem=s_XC[sx])
                P.pe(lambda e, sx=sx: e.transpose(out=pY[0][:, 0:128], in_=XB[sx][:], identity=ident_f[:]), reads=[r_XB[sx], r_const], writes=[r_pY[0]])
                for q in range(4):
                    P.act(lambda e, q=q, k8=k8, ri=ri: e.activation(out=Bt[32 * q:32 * q + 32, 4 * k8 + q, ri, :], in_=pY[0][32 * q:32 * q + 32, 0:128], func=AF.Copy),
                          reads=[r_pY[0]], writes=[r_BC])
                P.pe(lambda e, sx=sx: e.transpose(out=pY[1][:, 0:128], in_=XC[sx][:], identity=ident_f[:]), reads=[r_XC[sx], r_const], writes=[r_pY[1]])
                for q in range(4):
                    P.dve(lambda e, q=q, k8=k8, ri=ri: tsc(e, Ct[:, 4 * k8 + q, ri, 32 * q:32 * q + 32], pY[1][:, 32 * q:32 * q + 32], (1.0 if ri == 0 else -1.0), None, ALU.mult),
                          reads=[r_pY[1]], writes=[r_BC])
        LR = sb(p4a, [128, NP], F32, "LR"); LI = sb(p4a, [128, NP], F32, "LI"); LD = sb(p4a, [128, NP], F32, "LD"); r_lam = P.res()
        s_lam = P.new_dma_sem()
        P.dma("sp", LR[:], I["lam_re"][0].rearrange("(pi a) p -> (a p) pi", a=2), writes=[r_lam], sem=s_lam, allow_slow_non_contiguous=True)
        P.dma("sp", LI[:], I["lam_im"][0].rearrange("(pi a) p -> (a p) pi", a=2), writes=[r_lam], sem=s_lam, allow_slow_non_contiguous=True)
        ldv = I["log_dt"][0].rearrange("(pi a) -> a pi", a=2)
        for a in range(2):
            P.dma("sp", LD[64 * a:64 * a + 64, :], ldv[a:a + 1, :].broadcast_to([64, NP]), writes=[r_lam], sem=s_lam, allow_slow_non_contiguous=True)
        are = sb(p4a, [128, NP], F32, "are"); aim = sb(p4a, [128, NP], F32, "aim")
        fre = sb(p4a, [128, NP], F32, "fre"); fim = sb(p4a, [128, NP], F32, "fim")
        LTr = sb(p4a, [128, NP], F32, "LTr"); LTi = sb(p4a, [128, NP], F32, "LTi")
        sm = [sb(p4a, [128, NP], F32, f"sm{k}") for k in range(4)]
        r_s = P.res()
        iota = sb(p4a, [128, TCH], F32, "iota"); segm = sb(p4a, [128, 512], F32, "segm"); r_io = P.res(); s_io = P.new_dma_sem()
        P.dma("sp", iota[:], C["iota"], writes=[r_io], sem=s_io)
        P.dma("sp", segm[:], C["segm"][:, 0:512], writes=[r_io], sem=s_io)
        P.act(lambda e: e.activation(out=LD[:], in_=LD[:], func=AF.Exp), reads=[r_lam], writes=[r_lam])
        P.dve(lambda e: e.tensor_tensor(out=are[:], in0=LR[:], in1=LD[:], op=ALU.mult), reads=[r_lam], writes=[r_s])
        P.dve(lambda e: e.tensor_tensor(out=aim[:], in0=LI[:], in1=LD[:], op=ALU.mult), reads=[r_lam], writes=[r_s])
        with contextlib.ExitStack() as pt:
            NS = 8
            T_ = [sb(pt, [128, NS, TCH], F32, f"tt{k}") for k in range(6)]
            Ti = sb(pt, [128, NS, TCH], I32, "tti")
            r_T = P.res()
            io_b = iota[:].unsqueeze(1).broadcast_to([128, NS, TCH])
            for sl in range(NP // NS):
                p0 = sl * NS
                aim_b = aim[:, p0:p0 + NS].unsqueeze(2).broadcast_to([128, NS, TCH])
                are_b = are[:, p0:p0 + NS].unsqueeze(2).broadcast_to([128, NS, TCH])
                ang, ex, kf, s2, s4, em = T_
                rw = dict(reads=[r_T, r_s, r_io], writes=[r_T])
                P.dve(lambda e, aim_b=aim_b: e.tensor_tensor(out=ang[:], in0=aim_b, in1=io_b, op=ALU.mult), **rw)
                P.dve(lambda e, are_b=are_b: e.tensor_tensor(out=ex[:], in0=are_b, in1=io_b, op=ALU.mult), **rw)
                P.dve(lambda e: tsc(e, kf[:], ang[:], 1.0 / TWO_PI, 0.5, ALU.mult, ALU.add), **rw)
                P.dve(lambda e: e.tensor_copy(out=Ti[:], in_=kf[:]), **rw)
                P.dve(lambda e: e.tensor_copy(out=kf[:], in_=Ti[:]), **rw)
                P.dve(lambda e: e.scalar_tensor_tensor(out=ang[:], in0=kf[:], scalar=-TWO_PI, in1=ang[:], op0=ALU.mult, op1=ALU.add), **rw)
                P.act(lambda e: e.activation(out=s2[:], in_=ang[:], func=AF.Sin, scale=0.5), **rw)
                P.act(lambda e: e.activation(out=s4[:], in_=ang[:], func=AF.Sin, scale=0.25), **rw)
                P.dve(lambda e: e.tensor_tensor(out=s4[:], in0=s4[:], in1=s4[:], op=ALU.mult), **rw)
                P.dve(lambda e: tsc(e, s4[:], s4[:], -2.0, 1.0, ALU.mult, ALU.add), **rw)
                P.dve(lambda e: e.scalar_tensor_tensor(out=kf[:], in0=s2[:], scalar=2.0, in1=s4[:], op0=ALU.mult, op1=ALU.mult), **rw)
                P.dve(lambda e: e.tensor_tensor(out=s2[:], in0=s2[:], in1=s2[:], op=ALU.mult), **rw)
                P.dve(lambda e: tsc(e, s2[:], s2[:], -2.0, 1.0, ALU.mult, ALU.add), **rw)
                P.act(lambda e: e.activation(out=em[:], in_=ex[:], func=AF.Exp, scale=-1.0), **rw)
                P.act(lambda e: e.activation(out=ex[:], in_=ex[:], func=AF.Exp), **rw)
                rw2 = dict(reads=[r_T], writes=[r_tab])
                P.dve(lambda e, p0=p0: e.tensor_tensor(out=Pre[:, p0:p0 + NS, :], in0=ex[:], in1=s2[:], op=ALU.mult), **rw2)
                P.dve(lambda e, p0=p0: e.tensor_tensor(out=Pim[:, p0:p0 + NS, :], in0=ex[:], in1=kf[:], op=ALU.mult), **rw2)
                P.dve(lambda e, p0=p0: e.tensor_tensor(out=Fre[:, p0:p0 + NS, :], in0=em[:], in1=s2[:], op=ALU.mult), **rw2)
                P.dve(lambda e, p0=p0: e.scalar_tensor_tensor(out=Fim[:, p0:p0 + NS, :], in0=em[:], scalar=-1.0, in1=kf[:], op0=ALU.mult, op1=ALU.mult), **rw2)
            rws = dict(reads=[r_s, r_tab, r_lam], writes=[r_s])
            nr, ni, den, tq = sm
            P.dve(lambda e: tsc(e, nr[:], Pre[:, :, 1], -1.0, None, ALU.add), **rws)
            P.dve(lambda e: e.tensor_copy(out=ni[:], in_=Pim[:, :, 1]), **rws)
            P.dve(lambda e: e.tensor_tensor(out=den[:], in0=LR[:], in1=LR[:], op=ALU.mult), **rws)
            P.dve(lambda e: e.tensor_tensor(out=tq[:], in0=LI[:], in1=LI[:], op=ALU.mult), **rws)
            P.dve(lambda e: e.tensor_tensor(out=den[:], in0=den[:], in1=tq[:], op=ALU.add), **rws)
            P.dve(lambda e: e.reciprocal(out=den[:], in_=den[:]), **rws)
            P.dve(lambda e: e.tensor_tensor(out=fre[:], in0=nr[:], in1=LR[:], op=ALU.mult), **rws)
            P.dve(lambda e: e.tensor_tensor(out=tq[:], in0=ni[:], in1=LI[:], op=ALU.mult), **rws)
            P.dve(lambda e: e.tensor_tensor(out=fre[:], in0=fre[:], in1=tq[:], op=ALU.add), **rws)
            P.dve(lambda e: e.tensor_tensor(out=fre[:], in0=fre[:], in1=den[:], op=ALU.mult), **rws)
            P.dve(lambda e: e.tensor_tensor(out=fim[:], in0=ni[:], in1=LR[:], op=ALU.mult), **rws)
            P.dve(lambda e: e.tensor_tensor(out=tq[:], in0=nr[:], in1=LI[:], op=ALU.mult), **rws)
            P.dve(lambda e: e.tensor_tensor(out=fim[:], in0=fim[:], in1=tq[:], op=ALU.subtract), **rws)
            P.dve(lambda e: e.tensor_tensor(out=fim[:], in0=fim[:], in1=den[:], op=ALU.mult), **rws)
            P.dve(lambda e: e.tensor_tensor(out=LTr[:], in0=Pre[:, :, TCH - 1], in1=Pre[:, :, 1], op=ALU.mult), **rws)
            P.dve(lambda e: e.tensor_tensor(out=tq[:], in0=Pim[:, :, TCH - 1], in1=Pim[:, :, 1], op=ALU.mult), **rws)
            P.dve(lambda e: e.tensor_tensor(out=LTr[:], in0=LTr[:], in1=tq[:], op=ALU.subtract), **rws)
            P.dve(lambda e: e.tensor_tensor(out=LTi[:], in0=Pre[:, :, TCH - 1], in1=Pim[:, :, 1], op=ALU.mult), **rws)
            P.dve(lambda e: e.tensor_tensor(out=tq[:], in0=Pim[:, :, TCH - 1], in1=Pre[:, :, 1], op=ALU.mult), **rws)
            P.dve(lambda e: e.tensor_tensor(out=LTi[:], in0=LTi[:], in1=tq[:], op=ALU.add), **rws)
            for sl in range(NP // NS):
                p0 = sl * NS
                fr_b = fre[:, p0:p0 + NS].unsqueeze(2).broadcast_to([128, NS, TCH])
                fi_b = fim[:, p0:p0 + NS].unsqueeze(2).broadcast_to([128, NS, TCH])
                a_, b_, c_, d_ = T_[0], T_[1], T_[2], T_[3]
                rw3 = dict(reads=[r_T, r_s, r_tab], writes=[r_T])
                ER = Fre[:, p0:p0 + NS, :]; EI = Fim[:, p0:p0 + NS, :]
                P.dve(lambda e, fr_b=fr_b, ER=ER: e.tensor_tensor(out=a_[:], in0=ER, in1=fr_b, op=ALU.mult), **rw3)
                P.dve(lambda e, fi_b=fi_b, EI=EI: e.tensor_tensor(out=b_[:], in0=EI, in1=fi_b, op=ALU.mult), **rw3)
                P.dve(lambda e, fi_b=fi_b, ER=ER: e.tensor_tensor(out=c_[:], in0=ER, in1=fi_b, op=ALU.mult), **rw3)
                P.dve(lambda e, fr_b=fr_b, EI=EI: e.tensor_tensor(out=d_[:], in0=EI, in1=fr_b, op=ALU.mult), **rw3)
                rw4 = dict(reads=[r_T], writes=[r_tab])
                P.dve(lambda e, ER=ER: e.tensor_tensor(out=ER, in0=a_[:], in1=b_[:], op=ALU.subtract), **rw4)
                P.dve(lambda e, EI=EI: e.tensor_tensor(out=EI, in0=c_[:], in1=d_[:], op=ALU.add), **rw4)
            P.barrier()
        pxs.close()
        dsk = sb(p4, [128, 8], F32, "dsk"); gam = sb(p4, [128, 8], F32, "gam"); bgl = sb(p4, [128, 8], F32, "bgl"); r_dsk = P.res(); s_dsk = P.new_dma_sem()
        P.dma("sp", dsk[:], I["d_skip"][0].rearrange("(c p) -> p c", p=128), writes=[r_dsk], sem=s_dsk, allow_slow_non_contiguous=True)
        P.dma("sp", gam[:], I["out_norm_ssm"][0].rearrange("(c p) -> p c", p=128), writes=[r_dsk], sem=s_dsk, allow_slow_non_contiguous=True)
        P.dma("sp", bgl[:], I["b_glu"][0].rearrange("(c p) -> p c", p=128), writes=[r_dsk], sem=s_dsk, allow_slow_non_contiguous=True)
        G4 = sb(p4a, [128, 4, 2, L], BF16, "G4"); r_G4 = [P.res() for _ in range(4)]
        hT = sb(p4a, [128, 2, L], BF16, "hT"); r_hT = P.res()
        A1 = sb(p4, [128, 512], F32, "A1"); A2 = sb(p4, [128, 512], F32, "A2"); r_A = P.res()
        TB = [[sb(p4a, [128, 512], F32, "TB") for _ in range(6)] for _ in range(2)]
        r_TB = [[P.res() for _ in range(6)] for _ in range(2)]
        cin4 = sb(p4a, [128, NCH, 4, 2], F32, "cin4"); r_cin = P.res()
        Dt = sb(p4a, [128, 4, 2], F32, "Dt"); T1c = sb(p4a, [128, 4, 2], F32, "T1c"); T2c = sb(p4a, [128, 4, 2], F32, "T2c"); r_ch = P.res()
        Ma = sb(p4a, [128, 4, 2], F32, "Ma"); Mb = sb(p4a, [128, 4, 2], F32, "Mb"); r_M = P.res()
        nLTi = sb(p4a, [128, NP], F32, "nLTi")
        P.dve(lambda e: tsc(e, nLTi[:], LTi[:], -1.0, None, ALU.mult), reads=[r_s], writes=[r_s])
        gT = uT
        NFULL = 2048 // TCH
        tbc = 0
        for k in range(8):
            steps = [(q, bi) for q in range(4) for bi in range(len(BANKS))]
            pend = None
            for (q, bi) in steps:
                pi_ = 4 * k + q
                b0, b1 = BANKS[bi]; N = b1 - b0
                nc_ = max(N // TCH, 1); tw = min(TCH, N)
                tb = tbc % 2; tbc += 1
                T_ = TB[tb]; rT = r_TB[tb]
                for ri in range(2):
                    P.pe(lambda e, ri=ri, pi_=pi_, k=k, b0=b0, b1=b1, N=N: e.matmul(out=pB[ri][:, 0:N], lhsT=Bt[:, pi_, ri, :], rhs=uT[:, k, b0:b1], start=True, stop=True),
                         reads=[r_BC, r_uT[k]], writes=[r_pB[ri]])
                fr = Fre[:, pi_, 0:tw].unsqueeze(1).broadcast_to([128, nc_, tw])
                fi = Fim[:, pi_, 0:tw].unsqueeze(1).broadcast_to([128, nc_, tw])
                v3 = (lambda nc_: (lambda ap: ap.rearrange("p (c t) -> p c t", c=nc_)))(nc_)
                for j, (src, tab) in enumerate(((0, fr), (1, fi), (1, fr), (0, fi))):
                    P.dve(lambda e, j=j, src=src, tab=tab, N=N, v3=v3, T_=T_: e.tensor_tensor(out=v3(T_[j][:, 0:N]), in0=v3(pB[src][:, 0:N]), in1=tab, op=ALU.mult),
                          reads=[r_pB[src], r_tab], writes=[rT[j]])
                P.pool(lambda e, N=N, T_=T_: e.tensor_tensor(out=T_[4][:, 0:N], in0=T_[0][:, 0:N], in1=T_[1][:, 0:N], op=ALU.subtract), reads=[rT[0], rT[1]], writes=[rT[4]])
                P.pool(lambda e, N=N, T_=T_: e.tensor_tensor(out=T_[5][:, 0:N], in0=T_[2][:, 0:N], in1=T_[3][:, 0:N], op=ALU.add), reads=[rT[2], rT[3]], writes=[rT[5]])

                def scans(q=q, b0=b0, b1=b1, N=N, T_=T_, rT=rT):
                    for ri in range(2):
                        P.dve(lambda e, ri=ri: e.tensor_tensor_scan(out=G4[:, q, ri, b0:b1], data0=segm[:, 0:N], data1=T_[4 + ri][:, 0:N], initial=0.0, op0=ALU.mult, op1=ALU.add),
                              reads=[rT[4 + ri], r_io], writes=[r_G4[q]])
                if pend is not None:
                    pend()
                pend = scans
            pend()
            P.dve(lambda e, k=k: e.tensor_copy(out=Ma[:, :, 0], in_=LTr[:, 4 * k:4 * k + 4]), reads=[r_s, r_ch], writes=[r_M])
            P.dve(lambda e, k=k: e.tensor_copy(out=Ma[:, :, 1], in_=LTr[:, 4 * k:4 * k + 4]), reads=[r_s], writes=[r_M])
            P.dve(lambda e, k=k: e.tensor_copy(out=Mb[:, :, 0], in_=LTi[:, 4 * k:4 * k + 4]), reads=[r_s], writes=[r_M])
            P.dve(lambda e, k=k: e.tensor_copy(out=Mb[:, :, 1], in_=nLTi[:, 4 * k:4 * k + 4]), reads=[r_s], writes=[r_M])
            P.dve(lambda e: e.memset(cin4[:, 0, :, :], 0.0), reads=[r_cin], writes=[r_cin])
            for c in range(NCH - 1):
                te = TCH * c + TCH - 1
                P.dve(lambda e, c=c, te=te: e.tensor_tensor(out=Dt[:], in0=G4[:, :, :, te], in1=cin4[:, c, :, :], op=ALU.add), reads=r_G4 + [r_cin, r_ch], writes=[r_ch])
                P.dve(lambda e: e.tensor_tensor(out=T1c[:], in0=Dt[:], in1=Ma[:], op=ALU.mult), reads=[r_ch, r_M], writes=[r_ch])
                P.dve(lambda e: e.tensor_tensor(out=T2c[:], in0=Dt[:], in1=Mb[:], op=ALU.mult), reads=[r_ch, r_M], writes=[r_ch])
                P.dve(lambda e, c=c: e.tensor_tensor(out=cin4[:, c + 1, :, 0], in0=T1c[:, :, 0], in1=T2c[:, :, 1], op=ALU.add), reads=[r_ch], writes=[r_cin])
                P.dve(lambda e, c=c: e.tensor_tensor(out=cin4[:, c + 1, :, 1], in0=T1c[:, :, 1], in1=T2c[:, :, 0], op=ALU.add), reads=[r_ch], writes=[r_cin])
            for (q, bi) in steps:
                pi_ = 4 * k + q
                b0, b1 = BANKS[bi]; N = b1 - b0
                nc_ = max(N // TCH, 1); tw = min(TCH, N)
                c0 = b0 // TCH
                tb = tbc % 2; tbc += 1
                T_ = TB[tb]; rT = r_TB[tb]
                pr = Pre[:, pi_, 0:tw].unsqueeze(1).broadcast_to([128, nc_, tw])
                pim_ = Pim[:, pi_, 0:tw].unsqueeze(1).broadcast_to([128, nc_, tw])
                v3 = (lambda nc_: (lambda ap: ap.rearrange("p (c t) -> p c t", c=nc_)))(nc_)
                for ri in range(2):
                    for cc in range(nc_):
                        P.act(lambda e, ri=ri, q=q, b0=b0, cc=cc, tw=tw, c0=c0, T_=T_: e.activation(out=T_[ri][:, cc * tw:(cc + 1) * tw], in_=G4[:, q, ri, b0 + cc * tw:b0 + (cc + 1) * tw],
                                                                                          func=AF.Identity, bias=cin4[:, c0 + cc, q, ri:ri + 1], scale=1.0),
                              reads=[r_G4[q], r_cin], writes=[rT[ri]])
                for j, (src, tab) in enumerate(((0, pr), (1, pim_), (0, pim_), (1, pr))):
                    eng_ = P.pool if j == 3 else P.dve
                    eng_(lambda e, j=j, src=src, tab=tab, N=N, v3=v3, T_=T_: e.tensor_tensor(out=v3(T_[2 + j][:, 0:N]), in0=v3(T_[src][:, 0:N]), in1=tab, op=ALU.mult),
                         reads=[rT[src], r_tab], writes=[rT[2 + j]])
                P.pool(lambda e, b0=b0, b1=b1, N=N, T_=T_: e.tensor_tensor(out=hT[:, 0, b0:b1], in0=T_[2][:, 0:N], in1=T_[3][:, 0:N], op=ALU.subtract), reads=[rT[2], rT[3]], writes=[r_hT])
                P.pool(lambda e, b0=b0, b1=b1, N=N, T_=T_: e.tensor_tensor(out=hT[:, 1, b0:b1], in0=T_[4][:, 0:N], in1=T_[5][:, 0:N], op=ALU.add), reads=[rT[4], rT[5]], writes=[r_hT])
                for ri in range(2):
                    P.pe(lambda e, ri=ri, bi=bi, pi_=pi_, q=q, b0=b0, b1=b1, N=N: e.matmul(out=pY[bi][:, 0:N], lhsT=Ct[:, pi_, ri, :], rhs=hT[:, ri, b0:b1],
                                                                                 start=(q == 0 and ri == 0), stop=(q == 3 and ri == 1)),
                         reads=[r_BC, r_hT], writes=[r_pY[bi]])
            for bi, (b0, b1) in enumerate(BANKS):
                N = b1 - b0
                P.dve(lambda e, bi=bi, k=k, b0=b0, b1=b1, N=N: e.scalar_tensor_tensor(out=A1[:, 0:N], in0=uT[:, k, b0:b1], scalar=dsk[:, k:k + 1], in1=pY[bi][:, 0:N], op0=ALU.mult, op1=ALU.add),
                      reads=[r_uT[k], r_dsk, r_pY[bi]], writes=[r_A])
                P.dve(lambda e, N=N: e.tensor_tensor(out=A2[:, 0:N], in0=A1[:, 0:N], in1=A1[:, 0:N], op=ALU.mult), reads=[r_A], writes=[r_A])
                P.dve(lambda e, N=N: tsc(e, A2[:, 0:N], A2[:, 0:N], 0.044715, 1.0, ALU.mult, ALU.add), reads=[r_A], writes=[r_A])
                P.dve(lambda e, N=N: e.tensor_tensor(out=A2[:, 0:N], in0=A2[:, 0:N], in1=A1[:, 0:N], op=ALU.mult), reads=[r_A], writes=[r_A])
                P.act(lambda e, N=N: e.activation(out=A2[:, 0:N], in_=A2[:, 0:N], func=AF.Sigmoid, scale=1.5957691216057308), reads=[r_A], writes=[r_A])
                P.dve(lambda e, k=k, b0=b0, b1=b1, N=N: e.tensor_tensor(out=gT[:, k, b0:b1], in0=A1[:, 0:N], in1=A2[:, 0:N], op=ALU.mult), reads=[r_A], writes=[r_uT[k]])
        P.barrier()
        p4a.close()
        yagT = sb(mix, [128, 8, L], BF16, "yagT"); r_yag = P.res()
        ssa = sb(mix, [128, NT, 8], F32, "ssa"); r_ssa = P.res()
        P.pool(lambda e: e.memset(ssa[:], 1.0), writes=[r_ssa])
        wglu = sb(p4, [128, 8, DSSM], BF16, "wglu"); r_wglu = P.res(); s_wglu = P.new_dma_sem()
        P.dma("sp", wglu[:], wglu_bf, reads=[r_cv2], writes=[r_wglu], sem=s_wglu)
        GA1 = [sb(p4, [128, 512], F32, "GA1") for _ in range(2)]; GA2 = [sb(p4, [128, 512], F32, "GA2") for _ in range(2)]
        GX = [sb(p4, [128, 512], BF16, "GX") for _ in range(2)]
        r_GA = [P.res() for _ in range(2)]; r_GX = [P.res() for _ in range(2)]
        gstep = 0
        pend_g = None
        for fo in range(8):
            for bi, (b0, b1) in enumerate(BANKS):
                N = b1 - b0
                z = gstep % 2; gstep += 1
                for kc in range(8):
                    P.pe(lambda e, z=z, kc=kc, fo=fo, b0=b0, b1=b1, N=N: e.matmul(out=pB[z][:, 0:N], lhsT=wglu[:, kc, fo * 128:(fo + 1) * 128], rhs=gT[:, kc, b0:b1], start=(kc == 0), stop=(kc == 7)),
                         reads=[r_wglu] + r_uT, writes=[r_pB[z]])
                P.act(lambda e, z=z, fo=fo, N=N: e.activation(out=GA1[z][:, 0:N], in_=pB[z][:, 0:N], func=AF.Sigmoid, bias=bgl[:, fo:fo + 1], scale=1.0), reads=[r_pB[z], r_dsk], writes=[r_GA[z]])
                P.dve(lambda e, z=z, fo=fo, b0=b0, b1=b1, N=N: e.tensor_tensor(out=GA2[z][:, 0:N], in0=GA1[z][:, 0:N], in1=gT[:, fo, b0:b1], op=ALU.mult), reads=[r_GA[z], r_uT[fo]], writes=[r_GA[z]])
                P.dve(lambda e, z=z, fo=fo, b0=b0, b1=b1, N=N: tsc(e, yagT[:, fo, b0:b1], GA2[z][:, 0:N], gam[:, fo:fo + 1], None, ALU.mult), reads=[r_GA[z], r_dsk], writes=[r_yag])
                P.act(lambda e, z=z, N=N: e.activation(out=GX[z][:, 0:N], in_=GA2[z][:, 0:N], func=AF.Square), reads=[r_GA[z]], writes=[r_GX[z]])

                def gstats(z=z, fo=fo, b0=b0, b1=b1):
                    for i in range(NT):
                        if not (b0 <= 128 * i < b1):
                            continue
                        n = TS[i]
                        P.pe(lambda e, i=i, n=n: e.matmul(out=pst[0:n, i:i + 1], lhsT=GX[z][:, 128 * i - b0:128 * i - b0 + n], rhs=ones_bf[:, 0:1], start=True, stop=True),
                             reads=[r_GX[z], r_const], writes=[r_pst])
                        P.dve(lambda e, i=i, n=n: e.tensor_copy(out=ssa[0:n, i, fo:fo + 1], in_=pst[0:n, i:i + 1]), reads=[r_pst], writes=[r_ssa])
                if pend_g is not None:
                    pend_g()
                pend_g = gstats
        pend_g()
        P.barrier()

    if stage == 4:
        dump(yagT[:, 0, 0:512], r_yag, 512)
        dump(yagT[:, 7, 1552:2064], r_yag, 512)
        dump(yagT[:, 3, 512:1024], r_yag, 512)
        dump(ssa[:].rearrange("p a b -> p (a b)"), r_ssa, NT * 8)
        P.stopped = True

    rsa = sb(mix, [128, NT], F32, "rsa"); rsb = sb(mix, [128, NT], F32, "rsb"); r_rs = P.res()
    P.dve(lambda e: e.tensor_reduce(out=rsa[:], in_=ssa[:], axis=AX.X, op=ALU.add), reads=[r_ssa], writes=[r_rs])
    P.dve(lambda e: e.tensor_reduce(out=rsb[:], in_=ssb[:], axis=AX.X, op=ALU.add), reads=[r_ssb], writes=[r_rs])
    rstd_ops(rsa[:], r_rs, 1024)
    rstd_ops(rsb[:], r_rs, 1024)
    s_h1 = P.new_dma_sem()
    r_h1s = P.res("h1s")
    with contextlib.ExitStack() as p5:
        wo = [sb(p5, [128, 16, 512], BF16, "wo") for _ in range(2)]; r_wo = [P.res() for _ in range(2)]; s_wo = [P.new_dma_sem() for _ in range(2)]
        wo_v = I["w_out"][0].rearrange("(kc p) f -> p kc f", p=128)
        xr = [sb(p5, [128, 512], F32, "xr") for _ in range(2)]; r_xr = [P.res() for _ in range(2)]; s_xr = [P.new_dma_sem() for _ in range(2)]
        ho = [sb(p5, [128, 512], F32, "ho") for _ in range(2)]; r_ho = [P.res() for _ in range(2)]; s_ho = [P.new_dma_sem() for _ in range(2)]
        pA_ = [ps(p5, [128, 512], F32, "pA") for _ in range(2)]; r_pA = [P.pres() for _ in range(2)]
        pB_ = [ps(p5, [128, 512], F32, "pBo") for _ in range(2)]; r_pBo = [P.pres() for _ in range(2)]
        cnt = 0
        for cb in range(4):
            w = cb % 2
            P.dma("sp", wo[w][:], wo_bf[cb], reads=[r_cv2], writes=[r_wo[w]], sem=s_wo[w])
            for i in range(NT):
                n = TS[i]; s = cnt % 2; cnt += 1
                load_h_tile(xr[s], i, cb * 512, (cb + 1) * 512, r_xr[s], s_xr[s])
                for kc in range(8):
                    P.pe(lambda e, s=s, w=w, kc=kc, i=i, n=n: e.matmul(out=pA_[s][0:n, :], lhsT=yagT[:, kc, 128 * i:128 * i + n], rhs=wo[w][:, kc, :], start=(kc == 0), stop=(kc == 7)),
                         reads=[r_yag, r_wo[w]], writes=[r_pA[s]])
                for kc in range(8):
                    P.pe(lambda e, s=s, w=w, kc=kc, i=i, n=n: e.matmul(out=pB_[s][0:n, :], lhsT=ybgT[:, kc, 128 * i:128 * i + n], rhs=wo[w][:, 8 + kc, :], start=(kc == 0), stop=(kc == 7)),
                         reads=[r_ybg, r_wo[w]], writes=[r_pBo[s]])
                P.dve(lambda e, s=s, i=i, n=n: e.scalar_tensor_tensor(out=ho[s][0:n, :], in0=pA_[s][0:n, :], scalar=rsa[0:n, i:i + 1], in1=xr[s][0:n, :], op0=ALU.mult, op1=ALU.add),
                      reads=[r_pA[s], r_rs, r_xr[s]], writes=[r_ho[s]])
                P.dve(lambda e, s=s, i=i, n=n: e.scalar_tensor_tensor(out=ho[s][0:n, :], in0=pB_[s][0:n, :], scalar=rsb[0:n, i:i + 1], in1=ho[s][0:n, :], op0=ALU.mult, op1=ALU.add),
                      reads=[r_pBo[s], r_rs, r_ho[s]], writes=[r_ho[s]])
                P.dma("sp", h1s[128 * i:128 * i + n, cb * 512:(cb + 1) * 512], ho[s][0:n, :], reads=[r_ho[s]], writes=[r_h1s], sem=s_ho[s])
        P.barrier()
    mix.close()

    with contextlib.ExitStack() as p6:
        gffn = sb(p6, [128, D], F32, "gffn"); gfin = sb(p6, [128, D], F32, "gfin"); r_gf = P.res(); s_gf = P.new_dma_sem()
        P.dma("sp", gffn[:], I["ffn_norm"][0:1, :].broadcast_to([128, D]), writes=[r_gf], sem=s_gf)
        P.dma("sp", gfin[:], I["final_norm"].unsqueeze(0).broadcast_to([128, D]), writes=[r_gf], sem=s_gf)
        cw = sb(p6, [128, NFC, 3], F32, "cw"); cbias = sb(p6, [128, NFC], F32, "cbias")
        for kk in range(3):
            P.dma("sp", cw[:, :, kk], I["conv_w"][0][kk].rearrange("(fc p) -> p fc", p=128), writes=[r_gf], sem=s_gf, allow_slow_non_contiguous=True)
        P.dma("sp", cbias[:], I["conv_b"][0].rearrange("(fc p) -> p fc", p=128), writes=[r_gf], sem=s_gf, allow_slow_non_contiguous=True)
        halo = sb(p6, [128, NFC, 2], F32, "halo"); r_halo = P.res()
        P.pool(lambda e: e.memset(halo[:], 0.0), writes=[r_halo])
        h1g = sb(p6, [128, 5, D], F32, "h1g"); r_h1g = [P.res() for _ in range(5)]; s_h1g = [P.new_dma_sem() for _ in range(5)]
        xn2T = sb(p6, [128, 16, 528], BF16, "xn2T"); r_xn2T = P.res()
        actT = sb(p6, [128, NFC, 528], BF16, "actT"); r_actT = P.res()
        wup = [sb(p6, [128, 16, 256], BF16, "wup") for _ in range(2)]; r_wup = [P.res() for _ in range(2)]; s_wup = [P.new_dma_sem() for _ in range(2)]
        wdn = [sb(p6, [128, NFC, 128], BF16, "wdn") for _ in range(2)]; r_wdn = [P.res() for _ in range(2)]; s_wdn = [P.new_dma_sem() for _ in range(2)]
        wup_v = I["w_up"][0].rearrange("(kc p) f -> p kc f", p=128)
        wdn_v = I["w_down"][0].rearrange("(fc p) o -> p fc o", p=128)
        junk2 = sb(p6, [128, D], BF16, "junk2"); r_j2 = P.res()
        xnb2_ = [sb(p6, [128, D], BF16, "xnb2") for _ in range(2)]; r_xnb2_ = [P.res() for _ in range(2)]
        ss2 = sb(p6, [128, 16], F32, "ss2"); r_ss2_ = [P.res() for _ in range(16)]
        gbuf = sb(p6, [128, 516], F32, "gbuf"); r_gbuf = P.res()
        cbuf = sb(p6, [128, 512], F32, "cbuf"); r_cbuf = P.res()
        ffo = [sb(p6, [128, 512], F32, "ffo") for _ in range(2)]; r_ffo = [P.res() for _ in range(2)]
        ot = [sb(p6, [128, D], F32, "ot") for _ in range(2)]; r_ot = [P.res() for _ in range(2)]; s_ot = [P.new_dma_sem() for _ in range(2)]
        pT2 = [ps(p6, [128, 8, 128], BF16, "pT2") for _ in range(2)]; r_pT2 = [P.pres() for _ in range(2)]
        pG = [ps(p6, [128, 512], F32, "pG") for _ in range(2)]; r_pG = [P.pres() for _ in range(2)]
        pV = [ps(p6, [128, 512], F32, "pV") for _ in range(2)]; r_pV = [P.pres() for _ in range(2)]
        pTr = [ps(p6, [128, 512], F32, "pTr") for _ in range(2)]; r_pTr = [P.pres() for _ in range(2)]
        tc2 = 0; uc = 0; dc = 0; oc_ = 0; trcl = [0]
        for (t0, t1_) in GROUPS:
            g0 = 128 * t0; g1 = min(128 * t1_, L); GW = g1 - g0
            gbanks = [(0, min(GW, 512))] + ([(512, GW)] if GW > 512 else [])
            for ti, i in enumerate(range(t0, t1_)):
                n = TS[i]
                P.dma("sp", h1g[0:n, ti, :], h1s[128 * i:128 * i + n, :], reads=[r_h1s], writes=[r_h1g[ti]], sem=s_h1g[ti])
                P.act(lambda e, ti=ti, n=n: e.activation(out=junk2[0:n, :], in_=h1g[0:n, ti, :], func=AF.Square, accum_out=ss2[0:n, ti:ti + 1]), reads=[r_h1g[ti]], writes=[r_j2, r_ss2_[ti]])
                rstd_ops(ss2[0:n, ti:ti + 1], r_ss2_[ti], D)
                P.dve(lambda e, ti=ti, n=n: e.scalar_tensor_tensor(out=xnb2_[ti % 2][0:n, :], in0=h1g[0:n, ti, :], scalar=ss2[0:n, ti:ti + 1], in1=gffn[0:n, :], op0=ALU.mult, op1=ALU.mult),
                      reads=[r_h1g[ti], r_ss2_[ti], r_gf], writes=[r_xnb2_[ti % 2]])
                for k4 in range(4):
                    pp = tc2 % 2; tc2 += 1
                    for j in range(4):
                        kc = k4 * 4 + j
                        P.pe(lambda e, pp=pp, j=j, kc=kc, n=n, ti=ti: e.transpose(out=pT2[pp][:, j, 0:n], in_=xnb2_[ti % 2][0:n, kc * 128:(kc + 1) * 128], identity=ident_bf[0:n, 0:n]),
                             reads=[r_xnb2_[ti % 2], r_const], writes=[r_pT2[pp]])
                    P.act(lambda e, pp=pp, k4=k4, ti=ti, n=n: e.activation(out=xn2T[:, k4 * 4:k4 * 4 + 4, 128 * ti:128 * ti + n], in_=pT2[pp][:, 0:4, 0:n], func=AF.Copy),
                          reads=[r_pT2[pp]], writes=[r_xn2T])
            for fc in range(NFC):
                w = uc % 2; uc += 1
                P.dma("sp", wup[w][:], wup_bf[fc], reads=[r_cv], writes=[r_wup[w]], sem=s_wup[w])
                for (c0, c1) in gbanks:
                    N = c1 - c0
                    z = dc % 2; dc += 1
                    for kc in range(16):
                        P.pe(lambda e, z=z, w=w, kc=kc, c0=c0, c1=c1, N=N: e.matmul(out=pG[z][:, 0:N], lhsT=wup[w][:, kc, 0:128], rhs=xn2T[:, kc, c0:c1], start=(kc == 0), stop=(kc == 15)),
                             reads=[r_wup[w], r_xn2T], writes=[r_pG[z]])
                    for kc in range(16):
                        P.pe(lambda e, z=z, w=w, kc=kc, c0=c0, c1=c1, N=N: e.matmul(out=pV[z][:, 0:N], lhsT=wup[w][:, kc, 128:256], rhs=xn2T[:, kc, c0:c1], start=(kc == 0), stop=(kc == 15)),
                             reads=[r_wup[w], r_xn2T], writes=[r_pV[z]])
                    P.act(lambda e, fc=fc: e.activation(out=gbuf[:, 0:2], in_=halo[:, fc, :], func=AF.Copy), reads=[r_halo], writes=[r_gbuf])
                    P.act(lambda e, z=z, N=N: e.activation(out=gbuf[:, 2:2 + N], in_=pG[z][:, 0:N], func=AF.Copy), reads=[r_pG[z]], writes=[r_gbuf])
                    P.act(lambda e, fc=fc, N=N: e.activation(out=halo[:, fc, :], in_=gbuf[:, N:N + 2], func=AF.Copy), reads=[r_gbuf], writes=[r_halo])
                    P.dve(lambda e, fc=fc, N=N: tsc(e, cbuf[:, 0:N], gbuf[:, 2:2 + N], cw[:, fc, 2:3], cbias[:, fc:fc + 1], ALU.mult, ALU.add), reads=[r_gbuf, r_gf], writes=[r_cbuf])
                    P.dve(lambda e, fc=fc, N=N: e.scalar_tensor_tensor(out=cbuf[:, 0:N], in0=gbuf[:, 1:1 + N], scalar=cw[:, fc, 1:2], in1=cbuf[:, 0:N], op0=ALU.mult, op1=ALU.add), reads=[r_gbuf, r_gf, r_cbuf], writes=[r_cbuf])
                    P.dve(lambda e, fc=fc, N=N: e.scalar_tensor_tensor(out=cbuf[:, 0:N], in0=gbuf[:, 0:N], scalar=cw[:, fc, 0:1], in1=cbuf[:, 0:N], op0=ALU.mult, op1=ALU.add), reads=[r_gbuf, r_gf, r_cbuf], writes=[r_cbuf])
                    P.act(lambda e, N=N: e.activation(out=cbuf[:, 0:N], in_=cbuf[:, 0:N], func=AF.Silu), reads=[r_cbuf], writes=[r_cbuf])
                    P.dve(lambda e, z=z, fc=fc, c0=c0, c1=c1, N=N: e.tensor_tensor(out=actT[:, fc, c0:c1], in0=cbuf[:, 0:N], in1=pV[z][:, 0:N], op=ALU.mult), reads=[r_cbuf, r_pV[z]], writes=[r_actT])
            pend_c = None
            for oc in range(16):
                w = oc_ % 2; oc_ += 1
                P.dma("sp", wdn[w][:], wdn_bf[oc], reads=[r_cv], writes=[r_wdn[w]], sem=s_wdn[w])
                for (c0, c1) in gbanks:
                    N = c1 - c0
                    z = dc % 2; dc += 1
                    for fc in range(NFC):
                        P.pe(lambda e, z=z, w=w, fc=fc, c0=c0, c1=c1, N=N: e.matmul(out=pG[z][:, 0:N], lhsT=wdn[w][:, fc, :], rhs=actT[:, fc, c0:c1], start=(fc == 0), stop=(fc == NFC - 1)),
                             reads=[r_wdn[w], r_actT], writes=[r_pG[z]])
                    P.act(lambda e, z=z, N=N: e.activation(out=ffo[z][:, 0:N], in_=pG[z][:, 0:N], func=AF.Copy), reads=[r_pG[z]], writes=[r_ffo[z]])

                    def trs(z=z, c0=c0, c1=c1, oc=oc, t0=t0, t1_=t1_):
                        nonlocal_trc = trcl
                        for ti, i in enumerate(range(t0, t1_)):
                            if not (c0 <= 128 * ti < c1):
                                continue
                            n = TS[i]; q = nonlocal_trc[0] % 2; nonlocal_trc[0] += 1
                            lo = 128 * ti - c0
                            P.pe(lambda e, q=q, z=z, lo=lo, n=n: e.transpose(out=pTr[q][0:n, 0:128], in_=ffo[z][:, lo:lo + n], identity=ident_f[:]),
                                 reads=[r_ffo[z], r_const], writes=[r_pTr[q]])
                            P.dve(lambda e, q=q, ti=ti, n=n, oc=oc: e.tensor_tensor(out=h1g[0:n, ti, oc * 128:(oc + 1) * 128], in0=h1g[0:n, ti, oc * 128:(oc + 1) * 128], in1=pTr[q][0:n, 0:128], op=ALU.add),
                                  reads=[r_pTr[q], r_h1g[ti]], writes=[r_h1g[ti]])
                    if pend_c is not None:
                        pend_c()
                    pend_c = trs
            pend_c()
            for ti, i in enumerate(range(t0, t1_)):
                n = TS[i]; o = ti % 2
                P.act(lambda e, ti=ti, n=n: e.activation(out=junk2[0:n, :], in_=h1g[0:n, ti, :], func=AF.Square, accum_out=ss2[0:n, 8 + ti:9 + ti]), reads=[r_h1g[ti]], writes=[r_j2, r_ss2_[8 + ti]])
                rstd_ops(ss2[0:n, 8 + ti:9 + ti], r_ss2_[8 + ti], D)
                P.dve(lambda e, ti=ti, n=n, o=o: e.scalar_tensor_tensor(out=ot[o][0:n, :], in0=h1g[0:n, ti, :], scalar=ss2[0:n, 8 + ti:9 + ti], in1=gfin[0:n, :], op0=ALU.mult, op1=ALU.mult),
                      reads=[r_h1g[ti], r_ss2_[8 + ti], r_gf], writes=[r_ot[o]])
                if i == 0:
                    P.dma("sp", out[0:112, :], ot[o][16:128, :], reads=[r_ot[o]], sem=s_ot[o])
                else:
                    P.dma("sp", out[128 * i - 16:128 * i - 16 + n, :], ot[o][0:n, :], reads=[r_ot[o]], sem=s_ot[o])
        P.barrier()
    P.stopped = False
    P.barrier(final=True)
    P.emit(top)
    top.close()
    return nc


_NC_CACHE = {}


def kernel(**inputs):
    consts = host_consts()
    if "nc" not in _NC_CACHE:
        _NC_CACHE["nc"] = build_nc()
    nc = _NC_CACHE["nc"]
    x = np.asarray(inputs["x"], dtype=np.float32)
    B = x.shape[0]
    in_maps = []
    shared = {}
    for name, shape in IN_SPECS:
        if name == "x":
            continue
        shared[name] = np.ascontiguousarray(np.asarray(inputs[name], dtype=np.float32).reshape(shape))
    for name, shape, dt in CONST_SPECS:
        shared["c_" + name] = consts[name]
    for b in range(B):
        m = dict(shared)
        m["x"] = np.ascontiguousarray(x[b])
        in_maps.append(m)
    res = run_bass_kernel_spmd(nc, in_maps, core_ids=list(range(B)))
    return np.stack([np.asarray(r["out"], dtype=np.float32) for r in res.results], axis=0)
```

```python
import contextlib
import math
import os
import numpy as np
import ml_dtypes
from concourse.bass_utils import run_bass_kernel_spmd

import concourse.bass as bass
import concourse.mybir as mybir

ENGS = ("pe", "dve", "act", "pool", "sp")


class Res:
    __slots__ = ("name", "last_w", "rd_eng", "rd_dma", "excl")

    def __init__(self, name, excl=False):
        self.name = name
        self.excl = excl
        self.last_w = None
        self.rd_eng = {}
        self.rd_dma = []


class Op:
    __slots__ = ("eng", "fn", "waits", "inc", "dma_tok", "idx")


class Prog:
    def __init__(self, nc):
        self.nc = nc
        self.ops = {e: [] for e in ENGS}
        self.known = {e: {} for e in ENGS}
        self.dma_sems = []
        self.nres = 0
        self.stopped = False

    def res(self, name=None):
        self.nres += 1
        return Res(name or f"r{self.nres}")

    def pres(self, name=None):
        self.nres += 1
        return Res(name or f"p{self.nres}", excl=True)

    def new_dma_sem(self, background=False):
        self.dma_sems.append([None, 0, background])
        return len(self.dma_sems) - 1

    def _add(self, eng, fn, reads, writes, dma_sem=None):
        if self.stopped:
            return None
        ex = [r for r in reads if r.excl]
        if ex:
            writes = list(writes) + [r for r in ex if r not in writes]
        op = Op()
        op.eng = eng
        op.fn = fn
        op.inc = False
        op.dma_tok = None
        op.idx = len(self.ops[eng])
        deps = []
        for r in reads:
            if r.last_w is not None:
                deps.append(r.last_w)
        for w in writes:
            if w.last_w is not None:
                if not (dma_sem is not None and w.last_w[0] == "d" and w.last_w[1] == dma_sem):
                    deps.append(w.last_w)
            for e, i in w.rd_eng.items():
                deps.append(("e", e, i))
            deps.extend(w.rd_dma)
        if dma_sem is not None:
            s = self.dma_sems[dma_sem]
            s[1] += 16
            tok = ("d", dma_sem, s[1])
            op.dma_tok = tok
        else:
            tok = ("e", eng, op.idx)
        need = {}
        for d in deps:
            if d[0] == "e":
                if d[1] == eng and eng == "pe":
                    continue
                key = ("e", d[1])
                v = d[2]
            else:
                key = ("d", d[1])
                v = d[2]
            if v > need.get(key, -1):
                need[key] = v
        waits = []
        kn = self.known[eng]
        for key, v in need.items():
            if kn.get(key, -1) >= v:
                continue
            kn[key] = v
            waits.append((key, v))
            if key[0] == "e":
                self.ops[key[1]][v].inc = True
        op.waits = waits
        for r in reads:
            if tok[0] == "e":
                if r.rd_eng.get(eng, -1) < op.idx:
                    r.rd_eng[eng] = op.idx
            else:
                r.rd_dma.append(tok)
        for w in writes:
            w.last_w = tok
            w.rd_eng = {}
            w.rd_dma = []
        self.ops[eng].append(op)
        return op

    def pe(self, fn, reads=(), writes=()):
        return self._add("pe", fn, reads, writes)

    def dve(self, fn, reads=(), writes=()):
        return self._add("dve", fn, reads, writes)

    def act(self, fn, reads=(), writes=()):
        return self._add("act", fn, reads, writes)

    def pool(self, fn, reads=(), writes=()):
        return self._add("pool", fn, reads, writes)

    def dma(self, eng, out, in_, reads=(), writes=(), sem=None, **kw):
        assert sem is not None
        return self._add(eng, lambda e: e.dma_start(out=out, in_=in_, **kw), reads, writes, dma_sem=sem)

    def barrier(self, final=False):
        if self.stopped:
            return
        last = {e: len(self.ops[e]) - 1 for e in ENGS}
        for e in ENGS:
            op = Op()
            op.eng = e
            op.fn = None
            op.inc = False
            op.dma_tok = None
            op.idx = len(self.ops[e])
            waits = []
            kn = self.known[e]
            for f in ENGS:
                if f == e or last[f] < 0:
                    continue
                i = last[f]
                while i >= 0 and (self.ops[f][i].dma_tok is not None or self.ops[f][i].fn is None):
                    i -= 1
                if i < 0:
                    continue
                key = ("e", f)
                if kn.get(key, -1) >= i:
                    continue
                kn[key] = i
                waits.append((key, i))
                self.ops[f][i].inc = True
            for sid, s in enumerate(self.dma_sems):
                key = ("d", sid)
                if s[2] and not final:
                    continue
                if s[1] > 0 and kn.get(key, -1) < s[1]:
                    kn[key] = s[1]
                    waits.append((key, s[1]))
            op.waits = waits
            self.ops[e].append(op)

    def emit(self, stack):
        nc = self.nc
        esem = {e: stack.enter_context(nc.semaphore(f"es_{e}")) for e in ENGS}
        for i, s in enumerate(self.dma_sems):
            s[0] = stack.enter_context(nc.semaphore(f"ds_{i}"))
        val = {}
        for e in ENGS:
            c = 0
            for op in self.ops[e]:
                if op.inc:
                    c += 1
                    val[(e, op.idx)] = c
        if os.environ.get("KDBG_SYNC"):
            for e in ENGS:
                ops = self.ops[e]
                print("ENG", e, "nops", len(ops), "ninc", sum(1 for o in ops if o.inc))
                for o in ops[-6:]:
                    print("   idx", o.idx, "inc", o.inc, "val", val.get((e, o.idx)), "dma", o.dma_tok, "fn", o.fn is not None,
                          "waits", [(k, v, (val.get((k[1], v)) if k[0] == "e" else v)) for k, v in o.waits])
        block = stack.enter_context(nc.Block())
        bm = {"pe": block.tensor, "dve": block.vector, "act": block.scalar,
              "pool": block.gpsimd, "sp": block.sync}

        def run(e):
            def body(eng):
                for op in self.ops[e]:
                    for key, v in op.waits:
                        if key[0] == "e":
                            eng.wait_ge(esem[key[1]], val[(key[1], v)])
                        else:
                            eng.wait_ge(self.dma_sems[key[1]][0], v)
                    if op.fn is None:
                        continue
                    ins = op.fn(eng)
                    if op.dma_tok is not None:
                        ins.then_inc(self.dma_sems[op.dma_tok[1]][0], 16)
                    elif op.inc:
                        ins.then_inc(esem[e], 1)
            return body

        for e in ENGS:
            bm[e](run(e))


F32 = mybir.dt.float32
BF16 = mybir.dt.bfloat16
I32 = mybir.dt.int32
AF = mybir.ActivationFunctionType
ALU = mybir.AluOpType
AX = mybir.AxisListType

D = 2048
SEQ = 2048
NMETA = 16
L = SEQ + NMETA
NT = 17
TS = [128] * 16 + [16]
BANKS = [(0, 512), (512, 1024), (1024, 1536), (1536, 2048), (2048, 2064)]
DSSM = 1024
QL = 512
KVL = 256
DIN = 1856
H = 8
DFF = 5504
NFC = 43
EPS = 1e-6
SCALE = 1.0 / math.sqrt(192.0)
TCH = 64
NCH = 33
GROUPS = [(0, 4), (4, 8), (8, 12), (12, 17)]
TWO_PI = 2.0 * math.pi


def cid(pos):
    return 0 if pos < NMETA else 1 + (pos - NMETA) // 64


def chunk_start(c):
    return 0 if c == 0 else NMETA + 64 * (c - 1)


def host_consts():
    c = {}
    c["ident_bf"] = np.eye(128, dtype=np.float32).astype(ml_dtypes.bfloat16)
    c["ident_f"] = np.eye(128, dtype=np.float32)
    c["ones_bf"] = np.ones((128, 128), dtype=np.float32).astype(ml_dtypes.bfloat16)
    pos = np.arange(L, dtype=np.float32)
    inv_freq = (1.0 / (np.float32(10000.0) ** (np.arange(0, 64, 2, dtype=np.float32) / np.float32(64)))).astype(np.float32)
    ang = (pos[:, None] * inv_freq[None, :]).astype(np.float32)
    cos = np.cos(ang).astype(np.float32).T
    sin = np.sin(ang).astype(np.float32).T
    cosT = np.concatenate([cos, cos, cos, cos], axis=0)
    sinT = np.concatenate([-sin, sin, -sin, sin], axis=0)
    c["cosT"] = np.ascontiguousarray(cosT, dtype=np.float32)
    c["sinT"] = np.ascontiguousarray(sinT, dtype=np.float32)
    seg = np.ones((128, L), dtype=np.float32)
    seg[:, ::TCH] = 0.0
    c["segm"] = seg
    c["iota"] = np.tile(np.arange(TCH, dtype=np.float32)[None, :], (128, 1))
    m = np.zeros((128, NT, 128), dtype=np.float32)
    for kt in range(NT):
        k0 = 128 * kt
        qlo = chunk_start(cid(k0))
        for k in range(TS[kt]):
            ck = cid(k0 + k)
            for j in range(128):
                q = qlo + j
                if q < L and ck <= cid(q):
                    m[k, kt, j] = 1.0
    c["amask"] = ((m - 1.0) * 30000.0).astype(ml_dtypes.bfloat16)
    return c


CONST_SPECS = [("ident_bf", [128, 128], BF16), ("ident_f", [128, 128], F32), ("ones_bf", [128, 128], BF16),
               ("cosT", [128, L], F32), ("sinT", [128, L], F32), ("segm", [128, L], F32),
               ("iota", [128, TCH], F32), ("amask", [128, NT, 128], BF16)]

IN_SPECS = [("x", [SEQ, D]), ("meta_tokens", [NMETA, D]), ("mix_norm", [1, D]), ("w_in", [1, D, DIN]),
            ("lam_re", [1, 64, 64]), ("lam_im", [1, 64, 64]), ("log_dt", [1, 64]),
            ("b_re", [1, 64, 64, 16]), ("b_im", [1, 64, 64, 16]), ("c_re", [1, 64, 16, 64]), ("c_im", [1, 64, 16, 64]),
            ("d_skip", [1, DSSM]), ("w_glu", [1, DSSM, DSSM]), ("b_glu", [1, DSSM]),
            ("q_a_norm", [1, QL]), ("w_q_b", [1, QL, 1536]), ("kv_a_norm", [1, KVL]), ("w_kv_b", [1, KVL, 2048]),
            ("out_norm_ssm", [1, 1024]), ("out_norm_attn", [1, 1024]), ("w_out", [1, D, D]),
            ("ffn_norm", [1, D]), ("w_up", [1, D, 2 * DFF]), ("conv_w", [1, 3, DFF]), ("conv_b", [1, DFF]),
            ("w_down", [1, DFF, D]), ("final_norm", [D])]


class _Stop(Exception):
    pass


def build_nc(stage=99, dbg_shape=None):
    nc = bass.Bass("TRN2", target_bir_lowering=False)
    I = {}
    for name, shape in IN_SPECS:
        I[name] = nc.dram_tensor(name, shape, F32, kind="ExternalInput").ap()
    C = {}
    for name, shape, dt in CONST_SPECS:
        C[name] = nc.dram_tensor("c_" + name, shape, dt, kind="ExternalInput").ap()
    out = nc.dram_tensor("out", [SEQ, D], F32, kind="ExternalOutput").ap()
    h1s = nc.dram_tensor("h1s", [L, D], F32, kind="Internal").ap()
    wup_bf = nc.dram_tensor("wup_bf", [NFC, 128, 16, 256], BF16, kind="Internal").ap()
    wdn_bf = nc.dram_tensor("wdn_bf", [16, 128, NFC, 128], BF16, kind="Internal").ap()
    wglu_bf = nc.dram_tensor("wglu_bf", [128, 8, DSSM], BF16, kind="Internal").ap()
    wo_bf = nc.dram_tensor("wo_bf", [4, 128, 16, 512], BF16, kind="Internal").ap()
    dbg = None
    if dbg_shape is not None:
        dbg = nc.dram_tensor("dbg", dbg_shape, F32, kind="ExternalOutput").ap()

    P = Prog(nc)
    top = contextlib.ExitStack()
    top.__enter__()
    nameid = [0]

    ARENA = 207 * 1024
    arena = top.enter_context(nc.sbuf_tensor("arena", [128, ARENA // 2], BF16))
    free = [[0, ARENA]]
    peak = [0]

    def _alloc(nbytes):
        nbytes = (nbytes + 63) // 64 * 64
        for f in free:
            if f[1] - f[0] >= nbytes:
                off = f[0]
                f[0] += nbytes
                peak[0] = max(peak[0], off + nbytes)
                return off, nbytes
        raise RuntimeError(f"arena OOM need {nbytes} free={free}")

    def _release(off, nbytes):
        free.append([off, off + nbytes])
        free.sort()
        m = []
        for f in free:
            if f[0] == f[1]:
                continue
            if m and m[-1][1] == f[0]:
                m[-1][1] = f[1]
            else:
                m.append(f)
        free[:] = m

    def sb(stack, shape, dt, name=None):
        assert shape[0] == 128
        esz = 2 if dt == BF16 else 4
        n = 1
        for d_ in shape[1:]:
            n *= d_
        off, nb = _alloc(n * esz)
        v = arena[:, off // 2:off // 2 + n * esz // 2]
        if dt != BF16:
            v = v.bitcast(dt)
        if len(shape) == 3:
            v = v.rearrange("p (a b) -> p a b", a=shape[1])
        elif len(shape) == 4:
            v = v.rearrange("p (a b c) -> p a b c", a=shape[1], b=shape[2])
        stack.callback(_release, off, nb)
        if os.environ.get("KDBG_ALLOC"):
            print(f"ALLOC {name} off={off} nb={nb} stopped={P.stopped}")
        return v

    def ps(stack, shape, dt, name=None):
        nameid[0] += 1
        return stack.enter_context(nc.psum_tensor(f"{name or 'p'}_{nameid[0]}", shape, dt))

    def tsc(e, out_, in0, s1, s2, op0, op1=None):
        if op1 is None:
            return e.tensor_scalar(out=out_, in0=in0, scalar1=s1, scalar2=None, op0=op0)
        return e.tensor_scalar(out=out_, in0=in0, scalar1=s1, scalar2=s2, op0=op0, op1=op1)

    def hrows(i):
        if i == 0:
            return [(0, 16, ("meta", 0)), (16, 112, ("x", 0))]
        return [(0, TS[i], ("x", 128 * i - 16))]

    def load_h_tile(dst, i, c0, c1, wres, sem, eng="sp"):
        for (r0, n, (src, s0)) in hrows(i):
            ap = I["meta_tokens"] if src == "meta" else I["x"]
            P.dma(eng, dst[r0:r0 + n, 0:c1 - c0], ap[s0:s0 + n, c0:c1], writes=[wres], sem=sem)

    ident_bf = sb(top, [128, 128], BF16, "identb"); r_const = P.res("const")
    ident_f = sb(top, [128, 128], F32, "identf")
    ones_bf = sb(top, [128, 128], BF16, "ones")
    s_const = P.new_dma_sem()
    P.dma("sp", ident_bf[:], C["ident_bf"], writes=[r_const], sem=s_const)
    P.dma("sp", ident_f[:], C["ident_f"], writes=[r_const], sem=s_const)
    P.dma("sp", ones_bf[:], C["ones_bf"], writes=[r_const], sem=s_const)
    s_out = P.new_dma_sem()
    s_dbg = P.new_dma_sem()
    r_cv = P.res("wconv")
    r_cv2 = P.res("wconv2")

    dbg_col = [0]

    def dump(ap, rres, w, npart=128):
        with contextlib.ExitStack() as ds:
            stg = sb(ds, [128, w], F32, "dbgstg")
            rr = P.res()
            P.dve(lambda e: e.tensor_copy(out=stg[0:npart, :], in_=ap), reads=[rres], writes=[rr])
            c0 = dbg_col[0]
            dbg_col[0] += w
            P.dma("sp", dbg[0:npart, c0:c0 + w], stg[0:npart, :], reads=[rr], sem=s_dbg)
            P.barrier()

    def rstd_ops(t, r_t, n_feat):
        P.dve(lambda e: tsc(e, t, t, 1.0 / n_feat, EPS, ALU.mult, ALU.add), reads=[r_t], writes=[r_t])
        P.act(lambda e: e.activation(out=t, in_=t, func=AF.Sqrt), reads=[r_t], writes=[r_t])
        P.dve(lambda e: e.reciprocal(out=t, in_=t), reads=[r_t], writes=[r_t])

    ph = contextlib.ExitStack()
    xnT = sb(ph, [128, 16, L], BF16, "xnT"); r_xnT = [P.res(f"xnT{i}") for i in range(NT)]
    with contextlib.ExitStack() as p1:
        gmix = sb(p1, [128, D], F32, "gmix"); r_g = P.res()
        s_g = P.new_dma_sem()
        P.dma("sp", gmix[:], I["mix_norm"][0:1, :].broadcast_to([128, D]), writes=[r_g], sem=s_g)
        xt = [sb(p1, [128, D], F32, "xt") for _ in range(2)]; r_xt = [P.res() for _ in range(2)]
        s_xt = [P.new_dma_sem() for _ in range(2)]
        junk = sb(p1, [128, D], BF16, "junk"); r_junk = P.res()
        ssq = sb(p1, [128, NT], F32, "ssq"); r_ssq = [P.res() for _ in range(NT)]
        xnb = [sb(p1, [128, D], BF16, "xnb") for _ in range(2)]; r_xnb = [P.res() for _ in range(2)]
        pT = [ps(p1, [128, 8, 128], BF16, "pT") for _ in range(2)]; r_pT = [P.pres() for _ in range(2)]
        tcount = 0
        for i in range(NT):
            n = TS[i]; s = i % 2
            load_h_tile(xt[s], i, 0, D, r_xt[s], s_xt[s])
            P.act(lambda e, s=s, n=n, i=i: e.activation(out=junk[0:n, :], in_=xt[s][0:n, :], func=AF.Square, accum_out=ssq[0:n, i:i + 1]),
                  reads=[r_xt[s]], writes=[r_junk, r_ssq[i]])
            rstd_ops(ssq[0:n, i:i + 1], r_ssq[i], D)
            P.dve(lambda e, s=s, n=n, i=i: e.scalar_tensor_tensor(out=xnb[s][0:n, :], in0=xt[s][0:n, :], scalar=ssq[0:n, i:i + 1], in1=gmix[0:n, :], op0=ALU.mult, op1=ALU.mult),
                  reads=[r_xt[s], r_ssq[i], r_g], writes=[r_xnb[s]])
            for k4 in range(4):
                pp = tcount % 2; tcount += 1
                for j in range(4):
                    kc = k4 * 4 + j
                    P.pe(lambda e, pp=pp, j=j, kc=kc, s=s, n=n: e.transpose(out=pT[pp][:, j, 0:n], in_=xnb[s][0:n, kc * 128:(kc + 1) * 128], identity=ident_bf[0:n, 0:n]),
                         reads=[r_xnb[s], r_const], writes=[r_pT[pp]])
                P.act(lambda e, pp=pp, k4=k4, i=i, n=n: e.activation(out=xnT[:, k4 * 4:k4 * 4 + 4, 128 * i:128 * i + n], in_=pT[pp][:, 0:4, 0:n], func=AF.Copy),
                      reads=[r_pT[pp]], writes=[r_xnT[i]])
        P.barrier()
        if stage == 1:
            dump(xnT[:, 0, 0:512], r_xnT[0], 512)
            dump(xnT[:, 15, 1552:2064], r_xnT[16], 512)
            P.stopped = True

    mix = contextlib.ExitStack()
    mixA = contextlib.ExitStack()
    uT = sb(mix, [128, 8, L], BF16, "uT"); r_uT = [P.res(f"uT{c}") for c in range(8)]
    qagT = sb(mixA, [128, 4, L], BF16, "qagT"); r_qag = P.res()
    kvagT = sb(mixA, [128, 2, L], BF16, "kvagT"); r_kvag = P.res()
    KrT = sb(mixA, [128, L], BF16, "KrT"); r_KrT = P.res()
    KrTo = sb(mixA, [128, L], BF16, "KrTo")
    mixcs = contextlib.ExitStack()
    cosT = sb(mixcs, [128, L], F32, "cosT"); sinT = sb(mixcs, [128, L], F32, "sinT"); r_cs = P.res()
    s_cs = P.new_dma_sem()
    P.dma("sp", cosT[:], C["cosT"], writes=[r_cs], sem=s_cs)
    P.dma("sp", sinT[:], C["sinT"], writes=[r_cs], sem=s_cs)


    with contextlib.ExitStack() as p2:
        win_v = I["w_in"][0].rearrange("(kc p) f -> p kc f", p=128)
        wt = [sb(p2, [128, 16, 128], BF16, "wt") for _ in range(2)]; r_wt = [P.res() for _ in range(2)]
        s_wt = [P.new_dma_sem() for _ in range(2)]
        pz = [ps(p2, [128, 512], F32, "pz") for _ in range(3)]; r_pz = [P.pres() for _ in range(3)]
        sq_q = sb(p2, [128, 4, L], BF16, "sqq"); r_sqq = P.res()
        sq_kv = sb(p2, [128, 2, L], BF16, "sqkv"); r_sqkv = P.res()
        gq = sb(p2, [128, 4], F32, "gq"); gkv = sb(p2, [128, 2], F32, "gkv"); r_gq = P.res()
        s_gq = P.new_dma_sem()
        P.dma("sp", gq[:], I["q_a_norm"][0].rearrange("(c p) -> p c", p=128), writes=[r_gq], sem=s_gq, allow_slow_non_contiguous=True)
        P.dma("sp", gkv[:], I["kv_a_norm"][0].rearrange("(c p) -> p c", p=128), writes=[r_gq], sem=s_gq, allow_slow_non_contiguous=True)
        tmp1 = sb(p2, [128, 512], F32, "tmp1"); r_tmp1 = P.res()
        tmp2 = sb(p2, [128, 512], F32, "tmp2"); r_tmp2 = P.res()
        zc = 0
        chunks = [("u", c, [(0, 128, c * 128)]) for c in range(8)]
        chunks += [("q", c, [(0, 128, 1024 + c * 128)]) for c in range(4)]
        chunks += [("kv", c, [(0, 128, 1536 + c * 128)]) for c in range(2)]
        chunks += [("kpeA", 0, [(0, 64, 1792), (64, 64, 1792)]),
                   ("kpeB", 0, [(0, 32, 1824), (32, 32, 1792), (64, 32, 1824), (96, 32, 1792)])]
        all_xnT = list(r_xnT)
        pA = {}
        if stage == 20:
            chunks = chunks[0:1]
        if stage == 24:
            chunks = chunks[0:3]
        if stage == 26:
            chunks = chunks[0:2]
        if stage == 25:
            chunks = [chunks[0], chunks[8]]
        if stage == 21:
            chunks = chunks[0:9]
        if stage == 22:
            chunks = chunks[0:14]
        for ci, (kind, idx, pieces) in enumerate(chunks):
            s = ci % 2
            for (d0, ncol, s0) in pieces:
                P.dma("pool", wt[s][:, :, d0:d0 + ncol], win_v[:, :, s0:s0 + ncol], writes=[r_wt[s]], sem=s_wt[s])
            for bi, (b0, b1) in enumerate(BANKS):
                N = b1 - b0
                z = zc % 3; zc += 1
                for kc in range(16):
                    P.pe(lambda e, z=z, s=s, kc=kc, b0=b0, b1=b1, N=N: e.matmul(out=pz[z][:, 0:N], lhsT=wt[s][:, kc, :], rhs=xnT[:, kc, b0:b1], start=(kc == 0), stop=(kc == 15)),
                         reads=[r_wt[s]] + (all_xnT if kc == 0 else []), writes=[r_pz[z]])
                if kind == "u":
                    P.act(lambda e, z=z, idx=idx, b0=b0, b1=b1, N=N: e.activation(out=uT[:, idx, b0:b1], in_=pz[z][:, 0:N], func=AF.Copy),
                          reads=[r_pz[z]], writes=[r_uT[idx]])
                elif kind in ("q", "kv"):
                    dstT, sqT, gg, rr, rsq = (qagT, sq_q, gq, r_qag, r_sqq) if kind == "q" else (kvagT, sq_kv, gkv, r_kvag, r_sqkv)
                    P.dve(lambda e, z=z, idx=idx, b0=b0, b1=b1, N=N, dstT=dstT, gg=gg: tsc(e, dstT[:, idx, b0:b1], pz[z][:, 0:N], gg[:, idx:idx + 1], None, ALU.mult),
                          reads=[r_pz[z], r_gq], writes=[rr])
                    P.act(lambda e, z=z, idx=idx, b0=b0, b1=b1, N=N, sqT=sqT: e.activation(out=sqT[:, idx, b0:b1], in_=pz[z][:, 0:N], func=AF.Square),
                          reads=[r_pz[z]], writes=[rsq])
                elif kind == "kpeA":
                    pA[bi] = z
                    P.dve(lambda e, z=z, b0=b0, b1=b1, N=N: e.tensor_tensor(out=KrT[:, b0:b1], in0=pz[z][:, 0:N], in1=cosT[:, b0:b1], op=ALU.mult),
                          reads=[r_pz[z], r_cs], writes=[r_KrT])
                else:
                    P.dve(lambda e, z=z, b0=b0, b1=b1, N=N: e.tensor_tensor(out=tmp1[:, 0:N], in0=pz[z][:, 0:N], in1=sinT[:, b0:b1], op=ALU.mult),
                          reads=[r_pz[z], r_cs], writes=[r_tmp1])
                    P.dve(lambda e, b0=b0, b1=b1, N=N: e.tensor_tensor(out=KrT[:, b0:b1], in0=KrT[:, b0:b1], in1=tmp1[:, 0:N], op=ALU.add),
                          reads=[r_tmp1, r_KrT], writes=[r_KrT])
        if stage not in (20, 21, 24, 25, 26):
            P.pool(lambda e: e.memset(KrTo[0:64, :], 0.0), writes=[r_KrT])
            P.act(lambda e: e.activation(out=KrTo[64:128, :], in_=KrT[64:128, :], func=AF.Copy), reads=[r_KrT], writes=[r_KrT])
            P.pool(lambda e: e.memset(KrT[64:128, :], 0.0), reads=[r_KrT], writes=[r_KrT])
        P.barrier()
        if stage in (20, 21, 22, 23, 24, 25, 26):
            dump(uT[:, 0, 0:512], r_uT[0], 512)
            if stage in (21, 22, 23, 25):
                dump(qagT[:, 0, 0:512], r_qag, 512)
            if stage == 23:
                dump(KrT[:, 0:512], r_KrT, 512)
            P.stopped = True
        ph.close()
        rstd_q = sb(mixA, [128, L], F32, "rstdq"); r_rq = P.res()
        rstd_kv = sb(mixA, [128, L], F32, "rstdkv"); r_rkv = P.res()
        rkv_tok = sb(mixA, [128, NT], F32, "rkvtok"); r_rkt = P.res()
        for (sqT, nk, dst, rdst, rsq, nf) in ((sq_q, 4, rstd_q, r_rq, r_sqq, QL), (sq_kv, 2, rstd_kv, r_rkv, r_sqkv, KVL)):
            for (b0, b1) in BANKS:
                N = b1 - b0
                z = zc % 3; zc += 1
                for kc in range(nk):
                    P.pe(lambda e, z=z, kc=kc, b0=b0, b1=b1, N=N, sqT=sqT, nk=nk: e.matmul(out=pz[z][:, 0:N], lhsT=ones_bf[:], rhs=sqT[:, kc, b0:b1], start=(kc == 0), stop=(kc == nk - 1)),
                         reads=[rsq, r_const], writes=[r_pz[z]])
                P.act(lambda e, z=z, b0=b0, b1=b1, N=N, dst=dst: e.activation(out=dst[:, b0:b1], in_=pz[z][:, 0:N], func=AF.Copy),
                      reads=[r_pz[z]], writes=[rdst])
            rstd_ops(dst[:], rdst, nf)
        P.pool(lambda e: e.memset(rkv_tok[:], 1.0), writes=[r_rkt])
        z = zc % 3; zc += 1
        for i in range(NT):
            n = TS[i]
            for kc in range(2):
                P.pe(lambda e, z=z, i=i, n=n, kc=kc: e.matmul(out=pz[z][0:n, i:i + 1], lhsT=sq_kv[:, kc, 128 * i:128 * i + n], rhs=ones_bf[:, 0:1], start=(kc == 0), stop=(kc == 1)),
                     reads=[r_sqkv, r_const], writes=[r_pz[z]])
            P.dve(lambda e, z=z, i=i, n=n: e.tensor_copy(out=rkv_tok[0:n, i:i + 1], in_=pz[z][0:n, i:i + 1]), reads=[r_pz[z]], writes=[r_rkt])
        rstd_ops(rkv_tok[:], r_rkt, KVL)
        P.barrier()
        if stage == 2:
            dump(uT[:, 0, 0:512], r_uT[0], 512)
            dump(uT[:, 7, 1552:2064], r_uT[7], 512)
            dump(qagT[:, 0, 0:512], r_qag, 512)
            dump(kvagT[:, 1, 0:512], r_kvag, 512)
            dump(KrT[:, 0:512], r_KrT, 512)
            dump(rstd_q[:, 0:512], r_rq, 512)
            dump(rstd_kv[:, 1552:2064], r_rkv, 512)
            dump(rkv_tok[:], r_rkt, NT)
            P.stopped = True

    ybgT = sb(mix, [128, 8, L], BF16, "ybgT"); r_ybg = P.res()
    ssb = sb(mix, [128, NT, 8], F32, "ssb"); r_ssb = P.res()
    P.pool(lambda e: e.memset(ssb[:], 1.0), writes=[r_ssb])
    with contextlib.ExitStack() as p3:
        wq = sb(p3, [128, 4, 1536], BF16, "wq"); r_wq = P.res(); s_wq = P.new_dma_sem()
        wqA = sb(p3, [128, 4, 4, 128], BF16, "wqA"); wqB = sb(p3, [128, 4, 4, 128], BF16, "wqB")
        wkv = sb(p3, [128, 2, 2048], BF16, "wkv")
        wq_v = I["w_q_b"][0].rearrange("(kc p) f -> p kc f", p=128)
        P.dma("pool", wq[:], wq_v, writes=[r_wq], sem=s_wq)
        P.dma("pool", wkv[:], I["w_kv_b"][0].rearrange("(kc p) f -> p kc f", p=128), writes=[r_wq], sem=s_wq)
        for h in range(H):
            c0 = h * 192 + 128
            o = 64 * (h % 2)
            P.dma("pool", wqA[:, :, h // 2, o:o + 64], wq_v[:, :, c0:c0 + 64], writes=[r_wq], sem=s_wq)
            P.dma("pool", wqB[:, :, h // 2, o:o + 32], wq_v[:, :, c0 + 32:c0 + 64], writes=[r_wq], sem=s_wq)
            P.dma("pool", wqB[:, :, h // 2, o + 32:o + 64], wq_v[:, :, c0:c0 + 32], writes=[r_wq], sem=s_wq)
        gb = sb(p3, [128, 8], F32, "gb"); r_gb = P.res(); s_gb = P.new_dma_sem()
        P.dma("sp", gb[:], I["out_norm_attn"][0].rearrange("(c p) -> p c", p=128), writes=[r_gb], sem=s_gb, allow_slow_non_contiguous=True)
        QrT = sb(p3, [128, 4, L], BF16, "QrT"); r_QrT = P.res()
        t1 = sb(p3, [128, 512], F32, "t1"); r_t1 = P.res()
        t2 = sb(p3, [128, 512], F32, "t2"); r_t2 = P.res()
        mq = sb(p3, [128, 8], F32, "mq"); mk = sb(p3, [128, 8], F32, "mk"); negM = sb(p3, [128, 8], F32, "negM"); r_m = P.res()
        mtmp = sb(p3, [128, 1], F32, "mtmp"); r_mtmp = P.res()
        pq = ps(p3, [128, 512], F32, "pq"); r_pq = P.pres()
        pS = [ps(p3, [128, 512], F32, "pS") for _ in range(3)]; r_pS = [P.pres() for _ in range(3)]
        pO = [ps(p3, [128, 512], F32, "pO") for _ in range(2)]; r_pO = [P.pres() for _ in range(2)]
        pD = [ps(p3, [128, 512], F32, "pD") for _ in range(2)]; r_pD = [P.pres() for _ in range(2)]
        P.pool(lambda e: e.memset(mq[:], 0.0), writes=[r_m])
        P.pool(lambda e: e.memset(mk[:], 0.0), writes=[r_m])
        ac = 0
        for j in range(4):
            for (b0, b1) in BANKS:
                N = b1 - b0
                for (wsrc, z) in ((wqA, 0), (wqB, 1)):
                    for kc in range(4):
                        P.pe(lambda e, z=z, wsrc=wsrc, kc=kc, j=j, b0=b0, b1=b1, N=N: e.matmul(out=pS[z][:, 0:N], lhsT=wsrc[:, kc, j, :], rhs=qagT[:, kc, b0:b1], start=(kc == 0), stop=(kc == 3)),
                             reads=[r_wq, r_qag], writes=[r_pS[z]])
                P.dve(lambda e, b0=b0, b1=b1, N=N: e.tensor_tensor(out=t1[:, 0:N], in0=pS[0][:, 0:N], in1=cosT[:, b0:b1], op=ALU.mult), reads=[r_pS[0], r_cs], writes=[r_t1])
                P.dve(lambda e, b0=b0, b1=b1, N=N: e.tensor_tensor(out=t2[:, 0:N], in0=pS[1][:, 0:N], in1=sinT[:, b0:b1], op=ALU.mult), reads=[r_pS[1], r_cs], writes=[r_t2])
                P.dve(lambda e, N=N: e.tensor_tensor(out=t1[:, 0:N], in0=t1[:, 0:N], in1=t2[:, 0:N], op=ALU.add), reads=[r_t1, r_t2], writes=[r_t1])
                P.dve(lambda e, j=j, b0=b0, b1=b1, N=N: e.scalar_tensor_tensor(out=QrT[:, j, b0:b1], in0=t1[:, 0:N], scalar=SCALE, in1=rstd_q[:, b0:b1], op0=ALU.mult, op1=ALU.mult),
                      reads=[r_t1, r_rq], writes=[r_QrT])
        P.barrier()
        mixcs.close()
        s_cv = P.new_dma_sem(background=True)
        wupc_v = I["w_up"][0].rearrange("(kc p) f -> p kc f", p=128)
        wdnc_v = I["w_down"][0].rearrange("(fc p) o -> p fc o", p=128)
        s_cv2 = P.new_dma_sem(background=True)
        P.dma("pool", wglu_bf, I["w_glu"][0].rearrange("(kc p) f -> p kc f", p=128), writes=[r_cv2], sem=s_cv2)
        woc_v = I["w_out"][0].rearrange("(kc p) f -> p kc f", p=128)
        for cb in range(4):
            P.dma("pool", wo_bf[cb], woc_v[:, :, cb * 512:(cb + 1) * 512], writes=[r_cv2], sem=s_cv2)
        for fc in range(NFC):
            P.dma("pool", wup_bf[fc, :, :, 0:128], wupc_v[:, :, fc * 128:(fc + 1) * 128], writes=[r_cv], sem=s_cv)
            P.dma("pool", wup_bf[fc, :, :, 128:256], wupc_v[:, :, DFF + fc * 128:DFF + (fc + 1) * 128], writes=[r_cv], sem=s_cv)
        for oc in range(16):
            P.dma("pool", wdn_bf[oc], wdnc_v[:, :, oc * 128:(oc + 1) * 128], writes=[r_cv], sem=s_cv)
        amask = sb(p3, [128, NT, 128], BF16, "amask"); r_am = P.res()
        P.dma("sp", amask[:], C["amask"], writes=[r_am], sem=s_gb)
        QnT = [sb(p3, [128, L], BF16, "QnT") for _ in range(2)]; r_QnT = [P.res() for _ in range(2)]
        KnT = [sb(p3, [128, L], BF16, "KnT") for _ in range(2)]; r_KnT = [P.res() for _ in range(2)]
        Vh = [sb(p3, [128, NT, 128], BF16, "Vh") for _ in range(2)]; r_Vh = [P.res() for _ in range(2)]
        PT = [sb(p3, [128, 512], BF16, "PT") for _ in range(3)]; r_PT = [P.res() for _ in range(3)]
        sqt = sb(p3, [128, 512], BF16, "sqt"); r_sqt = P.res()
        sqr = sb(p3, [128, 512], BF16, "sqr"); r_sqr = P.res()
        from collections import deque
        acl = [0]
        obc = [0]
        pending = deque()
        prio = deque()
        sqp = sb(p3, [128, 512], BF16, "sqp"); r_sqp = P.res()
        sqrp2 = [sb(p3, [128, 512], BF16, "sqrp") for _ in range(2)]; r_sqrp = P.res()
        P.dve(lambda e: e.memset(sqrp2[0][:], 0.0), writes=[r_sqrp])
        P.dve(lambda e: e.memset(sqrp2[1][:], 0.0), writes=[r_sqrp])

        def proj_micro(h):
            s = h % 2
            ro = 64 * (h % 2)
            mp = []
            for (b0, b1) in BANKS:
                N = b1 - b0

                def g_q(b0=b0, b1=b1, N=N):
                    for kc in range(4):
                        P.pe(lambda e, kc=kc: e.matmul(out=pq[:, 0:N], lhsT=wq[:, kc, h * 192:h * 192 + 128], rhs=qagT[:, kc, b0:b1], start=(kc == 0), stop=(kc == 3)),
                             reads=[r_wq, r_qag], writes=[r_pq])
                    P.dve(lambda e: e.scalar_tensor_tensor(out=QnT[s][:, b0:b1], in0=pq[:, 0:N], scalar=SCALE, in1=rstd_q[:, b0:b1], op0=ALU.mult, op1=ALU.mult),
                          reads=[r_pq, r_rq], writes=[r_QnT[s]])

                def g_nq0(b0=b0, b1=b1, N=N):
                    P.act(lambda e: e.activation(out=sqp[:, 0:N], in_=QnT[s][:, b0:b1], func=AF.Square), reads=[r_QnT[s]], writes=[r_sqp])
                    P.act(lambda e: e.activation(out=sqrp2[h % 2][ro:ro + 64, 0:N], in_=QrT[ro:ro + 64, h // 2, b0:b1], func=AF.Square), reads=[r_QrT], writes=[r_sqrp])

                def g_nq(b0=b0, b1=b1, N=N):
                    P.pe(lambda e: e.matmul(out=pq[:, 0:N], lhsT=ones_bf[:], rhs=sqp[:, 0:N], start=True, stop=False), reads=[r_sqp, r_const], writes=[r_pq])
                    P.pe(lambda e: e.matmul(out=pq[:, 0:N], lhsT=ones_bf[:], rhs=sqrp2[h % 2][:, 0:N], start=False, stop=True), reads=[r_sqrp, r_const], writes=[r_pq])
                    P.dve(lambda e: e.tensor_reduce(out=mtmp[:], in_=pq[:, 0:N], axis=AX.X, op=ALU.max), reads=[r_pq], writes=[r_mtmp])
                    P.dve(lambda e: e.tensor_tensor(out=mq[:, h:h + 1], in0=mq[:, h:h + 1], in1=mtmp[:], op=ALU.max), reads=[r_mtmp, r_m], writes=[r_m])

                def g_k(b0=b0, b1=b1, N=N):
                    for kc in range(2):
                        P.pe(lambda e, kc=kc: e.matmul(out=pq[:, 0:N], lhsT=wkv[:, kc, h * 256:h * 256 + 128], rhs=kvagT[:, kc, b0:b1], start=(kc == 0), stop=(kc == 1)),
                             reads=[r_wq, r_kvag], writes=[r_pq])
                    P.dve(lambda e: e.tensor_tensor(out=KnT[s][:, b0:b1], in0=pq[:, 0:N], in1=rstd_kv[:, b0:b1], op=ALU.mult),
                          reads=[r_pq, r_rkv], writes=[r_KnT[s]])

                def g_nk0(b0=b0, b1=b1, N=N):
                    P.act(lambda e: e.activation(out=sqp[:, 0:N], in_=KnT[s][:, b0:b1], func=AF.Square), reads=[r_KnT[s]], writes=[r_sqp])
                    P.act(lambda e: e.activation(out=sqrp2[0][0:64, 0:N], in_=KrT[0:64, b0:b1], func=AF.Square), reads=[r_KrT], writes=[r_sqrp])

                def g_nk(b0=b0, b1=b1, N=N):
                    P.pe(lambda e: e.matmul(out=pq[:, 0:N], lhsT=ones_bf[:], rhs=sqp[:, 0:N], start=True, stop=False), reads=[r_sqp, r_const], writes=[r_pq])
                    P.pe(lambda e: e.matmul(out=pq[:, 0:N], lhsT=ones_bf[:], rhs=sqrp2[0][:, 0:N], start=False, stop=True), reads=[r_sqrp, r_const], writes=[r_pq])
                    P.dve(lambda e: e.tensor_reduce(out=mtmp[:], in_=pq[:, 0:N], axis=AX.X, op=ALU.max), reads=[r_pq], writes=[r_mtmp])
                    P.dve(lambda e: e.tensor_tensor(out=mk[:, h:h + 1], in0=mk[:, h:h + 1], in1=mtmp[:], op=ALU.max), reads=[r_mtmp, r_m], writes=[r_m])
                mp += [g_q, g_nq0, g_k, g_nq, g_nk0, g_nk]
            for i in range(NT):
                def g_v(i=i, n=TS[i]):
                    for kc in range(2):
                        P.pe(lambda e, kc=kc: e.matmul(out=pq[0:n, 0:128], lhsT=kvagT[:, kc, 128 * i:128 * i + n], rhs=wkv[:, kc, h * 256 + 128:h * 256 + 256], start=(kc == 0), stop=(kc == 1)),
                             reads=[r_wq, r_kvag], writes=[r_pq])
                    P.dve(lambda e: tsc(e, Vh[s][0:n, i, :], pq[0:n, 0:128], rkv_tok[0:n, i:i + 1], None, ALU.mult), reads=[r_pq, r_rkt], writes=[r_Vh[s]])
                mp.append(g_v)

            def g_m():
                P.dve(lambda e: e.tensor_tensor(out=negM[:, h:h + 1], in0=mq[:, h:h + 1], in1=mk[:, h:h + 1], op=ALU.mult), reads=[r_m], writes=[r_m])
                P.act(lambda e: e.activation(out=negM[:, h:h + 1], in_=negM[:, h:h + 1], func=AF.Sqrt), reads=[r_m], writes=[r_m])
                P.dve(lambda e: tsc(e, negM[:, h:h + 1], negM[:, h:h + 1], -1.0, None, ALU.mult), reads=[r_m], writes=[r_m])
            mp.append(g_m)
            return mp

        def attn_bank(h, b0, b1):
            s = h % 2
            ro = 64 * (h % 2)
            ob = obc[0] % 2; obc[0] += 1
            kts = [kt for kt in range(NT) if chunk_start(cid(128 * kt)) < b1]
            infos = []
            for kt in kts:
                nk = TS[kt]
                k0 = 128 * kt
                qlo = chunk_start(cid(k0))
                qfull = chunk_start(cid(min(k0 + 127, L - 1)))
                qs = max(qlo, b0); qe = b1
                a = acl[0] % 3; acl[0] += 1
                infos.append((kt, nk, k0, qlo, qfull, qs, qe, qe - qs, a))

            def qk(info):
                kt, nk, k0, qlo, qfull, qs, qe, W, a = info
                P.pe(lambda e: e.matmul(out=pS[a][0:nk, 0:W], lhsT=KnT[s][:, k0:k0 + nk], rhs=QnT[s][:, qs:qe], start=True, stop=False),
                     reads=[r_KnT[s], r_QnT[s]], writes=[r_pS[a]])
                pe_ = min(qfull, qe)
                part = pe_ > qs
                P.pe(lambda e: e.matmul(out=pS[a][0:nk, 0:W], lhsT=(KrT if ro == 0 else KrTo)[:, k0:k0 + nk], rhs=QrT[:, h // 2, qs:qe], start=False, stop=(not part)),
                     reads=[r_KrT, r_QrT], writes=[r_pS[a]])
                if part:
                    Wm = pe_ - qs; j0 = qs - qlo
                    P.pe(lambda e: e.matmul(out=pS[a][0:nk, 0:Wm], lhsT=ident_bf[0:nk, 0:nk], rhs=amask[0:nk, kt, j0:j0 + Wm], start=False, stop=True),
                         reads=[r_const, r_am], writes=[r_pS[a]])
                P.act(lambda e: e.activation(out=PT[a][0:nk, 0:W], in_=pS[a][0:nk, 0:W], func=AF.Exp, bias=negM[0:nk, h:h + 1], scale=1.0),
                      reads=[r_pS[a], r_m], writes=[r_PT[a]])

            def pvd(info, first, last):
                kt, nk, k0, qlo, qfull, qs, qe, W, a = info
                P.pe(lambda e: e.matmul(out=pO[ob][:, qs - b0:qs - b0 + W], lhsT=Vh[s][0:nk, kt, :], rhs=PT[a][0:nk, 0:W], start=first, stop=last),
                     reads=[r_Vh[s], r_PT[a]], writes=[r_pO[ob]])
                P.pe(lambda e: e.matmul(out=pD[ob][:, qs - b0:qs - b0 + W], lhsT=ones_bf[0:nk, :], rhs=PT[a][0:nk, 0:W], start=first, stop=last),
                     reads=[r_const, r_PT[a]], writes=[r_pD[ob]])

            DEPTH = 2
            for ii in range(len(infos) + DEPTH):
                if ii < len(infos):
                    qk(infos[ii])
                jj = ii - DEPTH
                if jj >= 0:
                    pvd(infos[jj], jj == 0, jj == len(infos) - 1)
                if prio:
                    prio.popleft()()
                if pending:
                    pending.popleft()()
            N = b1 - b0
            epi = []
            for c_ in range(0, N, 128):
                ce = min(c_ + 128, N)
                epi.append(lambda c_=c_, ce=ce: P.dve(lambda e: e.reciprocal(out=t2[:, c_:ce], in_=pD[ob][:, c_:ce]), reads=[r_pD[ob]], writes=[r_t2]))
            epi.append(lambda: P.dve(lambda e: e.tensor_tensor(out=t1[:, 0:N], in0=pO[ob][:, 0:N], in1=t2[:, 0:N], op=ALU.mult), reads=[r_pO[ob], r_t2], writes=[r_t1]))

            def epi_fin():
                P.dve(lambda e: tsc(e, ybgT[:, h, b0:b1], t1[:, 0:N], gb[:, h:h + 1], None, ALU.mult), reads=[r_t1, r_gb], writes=[r_ybg])
                P.act(lambda e: e.activation(out=sqt[:, 0:N], in_=t1[:, 0:N], func=AF.Square), reads=[r_t1], writes=[r_sqt])
            epi.append(epi_fin)
            stats = []
            for i in range(NT):
                if not (b0 <= 128 * i < b1):
                    continue

                def g_s(i=i, n=TS[i]):
                    P.pe(lambda e: e.matmul(out=pq[0:n, 256 + i:257 + i], lhsT=sqt[:, 128 * i - b0:128 * i - b0 + n], rhs=ones_bf[:, 0:1], start=True, stop=True),
                         reads=[r_sqt, r_const], writes=[r_pq])
                    P.dve(lambda e: e.tensor_copy(out=ssb[0:n, i, h:h + 1], in_=pq[0:n, 256 + i:257 + i]), reads=[r_pq], writes=[r_ssb])
                stats.append(g_s)
            prio.extend(epi)
            prio.extend([lambda: None, lambda: None])
            prio.extend(stats)

        for g_ in proj_micro(0):
            g_()
        for h in range(H):
            if h + 1 < H:
                pending.extend(proj_micro(h + 1))
            for (b0, b1) in BANKS:
                attn_bank(h, b0, b1)
            while prio or pending:
                if prio:
                    prio.popleft()()
                if pending:
                    pending.popleft()()
        P.barrier()
        if stage == 3:
            dump(ybgT[:, 0, 0:512], r_ybg, 512)
            dump(ybgT[:, 7, 1552:2064], r_ybg, 512)
            dump(ybgT[:, 3, 512:1024], r_ybg, 512)
            dump(ssb[:].rearrange("p a b -> p (a b)"), r_ssb, NT * 8)
            P.stopped = True

    mixA.close()

    with contextlib.ExitStack() as p4:
        NP = 32
        p4a = contextlib.ExitStack()
        pst = ps(p4, [128, 512], F32, "pst"); r_pst = P.pres()
        pY = [ps(p4, [128, 512], F32, "pY") for _ in range(5)]; r_pY = [P.pres() for _ in range(5)]
        pB = [ps(p4, [128, 512], F32, "pB") for _ in range(2)]; r_pB = [P.pres() for _ in range(2)]
        Fre = sb(p4a, [128, NP, TCH], F32, "Fre"); Fim = sb(p4a, [128, NP, TCH], F32, "Fim")
        Pre = sb(p4a, [128, NP, TCH], F32, "Pre"); Pim = sb(p4a, [128, NP, TCH], F32, "Pim"); r_tab = P.res()
        Bt = sb(p4a, [128, NP, 2, 128], BF16, "Bt"); Ct = sb(p4a, [128, NP, 2, 128], BF16, "Ct"); r_BC = P.res()
        s_bc = P.new_dma_sem()
        P.dve(lambda e: e.memset(Bt[:], 0.0), writes=[r_BC])
        P.dve(lambda e: e.memset(Ct[:], 0.0), writes=[r_BC])
        pxs = contextlib.ExitStack()
        XB = [sb(pxs, [128, 128], F32, "XB") for _ in range(8)]; r_XB = [P.res() for _ in range(8)]; s_XB = [P.new_dma_sem() for _ in range(8)]
        XC = [sb(pxs, [128, 128], F32, "XC") for _ in range(8)]; r_XC = [P.res() for _ in range(8)]; s_XC = [P.new_dma_sem() for _ in range(8)]
        for t_ in XB + XC:
            P.dve(lambda e, t_=t_: e.memset(t_[:], 0.0), writes=r_XB + r_XC)
        sc_ = 0
        for k8 in range(8):
            for ri, (bsrc, csrc) in enumerate(((I["b_re"], I["c_re"]), (I["b_im"], I["c_im"]))):
                sx = sc_ % 8; sc_ += 1
                for par in range(2):
                    src = bsrc[0, 8 * k8 + par:8 * k8 + 8:2].rearrange("j p c -> p j c")
                    dst = XB[sx][64 * par:64 * par + 64, :].rearrange("p (j c) -> p j c", c=16)[:, par::2, :]
                    P.dma("sp", dst, src, writes=[r_XB[sx]], sem=s_XB[sx])
                for j in range(8):
                    P.dma("sp", XC[sx][16 * j:16 * j + 16, 64 * (j % 2):64 * (j % 2) + 64], csrc[0, 8 * k8 + j], writes=[r_XC[sx]], sem=s_XC[sx])
                P.pe(lambda e, sx=sx: e.transpose(out=pY[0][:, 0:128], in_=XB[sx][:], identity=ident_f[:]), reads=[r_XB[sx], r_const], writes=[r_pY[0]])
                for q in range(4):
                    P.act(lambda e, q=q, k8=k8, ri=ri: e.activation(out=Bt[32 * q:32 * q + 32, 4 * k8 + q, ri, :], in_=pY[0][32 * q:32 * q + 32, 0:128], func=AF.Copy),
                          reads=[r_pY[0]], writes=[r_BC])
                P.pe(lambda e, sx=sx: e.transpose(out=pY[1][:, 0:128], in_=XC[sx][:], identity=ident_f[:]), reads=[r_XC[sx], r_const], writes=[r_pY[1]])
                for q in range(4):
                    P.dve(lambda e, q=q, k8=k8, ri=ri: tsc(e, Ct[:, 4 * k8 + q, ri, 32 * q:32 * q + 32], pY[1][:, 32 * q:32 * q + 32], (1.0 if ri == 0 else -1.0), None, ALU.mult),
                          reads=[r_pY[1]], writes=[r_BC])
        LR = sb(p4a, [128, NP], F32, "LR"); LI = sb(p4a, [128, NP], F32, "LI"); LD = sb(p4a, [128, NP], F32, "LD"); r_lam = P.res()
        s_lam = P.new_dma_sem()
        P.dma("sp", LR[:], I["lam_re"][0].rearrange("(pi a) p -> (a p) pi", a=2), writes=[r_lam], sem=s_lam, allow_slow_non_contiguous=True)
        P.dma("sp", LI[:], I["lam_im"][0].rearrange("(pi a) p -> (a p) pi", a=2), writes=[r_lam], sem=s_lam, allow_slow_non_contiguous=True)
        ldv = I["log_dt"][0].rearrange("(pi a) -> a pi", a=2)
        for a in range(2):
            P.dma("sp", LD[64 * a:64 * a + 64, :], ldv[a:a + 1, :].broadcast_to([64, NP]), writes=[r_lam], sem=s_lam, allow_slow_non_contiguous=True)
        are = sb(p4a, [128, NP], F32, "are"); aim = sb(p4a, [128, NP], F32, "aim")
        fre = sb(p4a, [128, NP], F32, "fre"); fim = sb(p4a, [128, NP], F32, "fim")
        LTr = sb(p4a, [128, NP], F32, "LTr"); LTi = sb(p4a, [128, NP], F32, "LTi")
        sm = [sb(p4a, [128, NP], F32, f"sm{k}") for k in range(4)]
        r_s = P.res()
        iota = sb(p4a, [128, TCH], F32, "iota"); segm = sb(p4a, [128, 512], F32, "segm"); r_io = P.res(); s_io = P.new_dma_sem()
        P.dma("sp", iota[:], C["iota"], writes=[r_io], sem=s_io)
        P.dma("sp", segm[:], C["segm"][:, 0:512], writes=[r_io], sem=s_io)
        P.act(lambda e: e.activation(out=LD[:], in_=LD[:], func=AF.Exp), reads=[r_lam], writes=[r_lam])
        P.dve(lambda e: e.tensor_tensor(out=are[:], in0=LR[:], in1=LD[:], op=ALU.mult), reads=[r_lam], writes=[r_s])
        P.dve(lambda e: e.tensor_tensor(out=aim[:], in0=LI[:], in1=LD[:], op=ALU.mult), reads=[r_lam], writes=[r_s])
        with contextlib.ExitStack() as pt:
            NS = 8
            T_ = [sb(pt, [128, NS, TCH], F32, f"tt{k}") for k in range(6)]
            Ti = sb(pt, [128, NS, TCH], I32, "tti")
            r_T = P.res()
            io_b = iota[:].unsqueeze(1).broadcast_to([128, NS, TCH])
            for sl in range(NP // NS):
                p0 = sl * NS
                aim_b = aim[:, p0:p0 + NS].unsqueeze(2).broadcast_to([128, NS, TCH])
                are_b = are[:, p0:p0 + NS].unsqueeze(2).broadcast_to([128, NS, TCH])
                ang, ex, kf, s2, s4, em = T_
                rw = dict(reads=[r_T, r_s, r_io], writes=[r_T])
                P.dve(lambda e, aim_b=aim_b: e.tensor_tensor(out=ang[:], in0=aim_b, in1=io_b, op=ALU.mult), **rw)
                P.dve(lambda e, are_b=are_b: e.tensor_tensor(out=ex[:], in0=are_b, in1=io_b, op=ALU.mult), **rw)
                P.dve(lambda e: tsc(e, kf[:], ang[:], 1.0 / TWO_PI, 0.5, ALU.mult, ALU.add), **rw)
                P.dve(lambda e: e.tensor_copy(out=Ti[:], in_=kf[:]), **rw)
                P.dve(lambda e: e.tensor_copy(out=kf[:], in_=Ti[:]), **rw)
                P.dve(lambda e: e.scalar_tensor_tensor(out=ang[:], in0=kf[:], scalar=-TWO_PI, in1=ang[:], op0=ALU.mult, op1=ALU.add), **rw)
                P.act(lambda e: e.activation(out=s2[:], in_=ang[:], func=AF.Sin, scale=0.5), **rw)
                P.act(lambda e: e.activation(out=s4[:], in_=ang[:], func=AF.Sin, scale=0.25), **rw)
                P.dve(lambda e: e.tensor_tensor(out=s4[:], in0=s4[:], in1=s4[:], op=ALU.mult), **rw)
                P.dve(lambda e: tsc(e, s4[:], s4[:], -2.0, 1.0, ALU.mult, ALU.add), **rw)
                P.dve(lambda e: e.scalar_tensor_tensor(out=kf[:], in0=s2[:], scalar=2.0, in1=s4[:], op0=ALU.mult, op1=ALU.mult), **rw)
                P.dve(lambda e: e.tensor_tensor(out=s2[:], in0=s2[:], in1=s2[:], op=ALU.mult), **rw)
                P.dve(lambda e: tsc(e, s2[:], s2[:], -2.0, 1.0, ALU.mult, ALU.add), **rw)
                P.act(lambda e: e.activation(out=em[:], in_=ex[:], func=AF.Exp, scale=-1.0), **rw)
                P.act(lambda e: e.activation(out=ex[:], in_=ex[:], func=AF.Exp), **rw)
                rw2 = dict(reads=[r_T], writes=[r_tab])
                P.dve(lambda e, p0=p0: e.tensor_tensor(out=Pre[:, p0:p0 + NS, :], in0=ex[:], in1=s2[:], op=ALU.mult), **rw2)
                P.dve(lambda e, p0=p0: e.tensor_tensor(out=Pim[:, p0:p0 + NS, :], in0=ex[:], in1=kf[:], op=ALU.mult), **rw2)
                P.dve(lambda e, p0=p0: e.tensor_tensor(out=Fre[:, p0:p0 + NS, :], in0=em[:], in1=s2[:], op=ALU.mult), **rw2)
                P.dve(lambda e, p0=p0: e.scalar_tensor_tensor(out=Fim[:, p0:p0 + NS, :], in0=em[:], scalar=-1.0, in1=kf[:], op0=ALU.mult, op1=ALU.mult), **rw2)
            rws = dict(reads=[r_s, r_tab, r_lam], writes=[r_s])
            nr, ni, den, tq = sm
            P.dve(lambda e: tsc(e, nr[:], Pre[:, :, 1], -1.0, None, ALU.add), **rws)
            P.dve(lambda e: e.tensor_copy(out=ni[:], in_=Pim[:, :, 1]), **rws)
            P.dve(lambda e: e.tensor_tensor(out=den[:], in0=LR[:], in1=LR[:], op=ALU.mult), **rws)
            P.dve(lambda e: e.tensor_tensor(out=tq[:], in0=LI[:], in1=LI[:], op=ALU.mult), **rws)
            P.dve(lambda e: e.tensor_tensor(out=den[:], in0=den[:], in1=tq[:], op=ALU.add), **rws)
            P.dve(lambda e: e.reciprocal(out=den[:], in_=den[:]), **rws)
            P.dve(lambda e: e.tensor_tensor(out=fre[:], in0=nr[:], in1=LR[:], op=ALU.mult), **rws)
            P.dve(lambda e: e.tensor_tensor(out=tq[:], in0=ni[:], in1=LI[:], op=ALU.mult), **rws)
            P.dve(lambda e: e.tensor_tensor(out=fre[:], in0=fre[:], in1=tq[:], op=ALU.add), **rws)
            P.dve(lambda e: e.tensor_tensor(out=fre[:], in0=fre[:], in1=den[:], op=ALU.mult), **rws)
            P.dve(lambda e: e.tensor_tensor(out=fim[:], in0=ni[:], in1=LR[:], op=ALU.mult), **rws)
            P.dve(lambda e: e.tensor_tensor(out=tq[:], in0=nr[:], in1=LI[:], op=ALU.mult), **rws)
            P.dve(lambda e: e.tensor_tensor(out=fim[:], in0=fim[:], in1=tq[:], op=ALU.subtract), **rws)
            P.dve(lambda e: e.tensor_tensor(out=fim[:], in0=fim[:], in1=den[:], op=ALU.mult), **rws)
            P.dve(lambda e: e.tensor_tensor(out=LTr[:], in0=Pre[:, :, TCH - 1], in1=Pre[:, :, 1], op=ALU.mult), **rws)
            P.dve(lambda e: e.tensor_tensor(out=tq[:], in0=Pim[:, :, TCH - 1], in1=Pim[:, :, 1], op=ALU.mult), **rws)
            P.dve(lambda e: e.tensor_tensor(out=LTr[:], in0=LTr[:], in1=tq[:], op=ALU.subtract), **rws)
            P.dve(lambda e: e.tensor_tensor(out=LTi[:], in0=Pre[:, :, TCH - 1], in1=Pim[:, :, 1], op=ALU.mult), **rws)
            P.dve(lambda e: e.tensor_tensor(out=tq[:], in0=Pim[:, :, TCH - 1], in1=Pre[:, :, 1], op=ALU.mult), **rws)
            P.dve(lambda e: e.tensor_tensor(out=LTi[:], in0=LTi[:], in1=tq[:], op=ALU.add), **rws)
            for sl in range(NP // NS):
                p0 = sl * NS
                fr_b = fre[:, p0:p0 + NS].unsqueeze(2).broadcast_to([128, NS, TCH])
                fi_b = fim[:, p0:p0 + NS].unsqueeze(2).broadcast_to([128, NS, TCH])
                a_, b_, c_, d_ = T_[0], T_[1], T_[2], T_[3]
                rw3 = dict(reads=[r_T, r_s, r_tab], writes=[r_T])
                ER = Fre[:, p0:p0 + NS, :]; EI = Fim[:, p0:p0 + NS, :]
                P.dve(lambda e, fr_b=fr_b, ER=ER: e.tensor_tensor(out=a_[:], in0=ER, in1=fr_b, op=ALU.mult), **rw3)
                P.dve(lambda e, fi_b=fi_b, EI=EI: e.tensor_tensor(out=b_[:], in0=EI, in1=fi_b, op=ALU.mult), **rw3)
                P.dve(lambda e, fi_b=fi_b, ER=ER: e.tensor_tensor(out=c_[:], in0=ER, in1=fi_b, op=ALU.mult), **rw3)
                P.dve(lambda e, fr_b=fr_b, EI=EI: e.tensor_tensor(out=d_[:], in0=EI, in1=fr_b, op=ALU.mult), **rw3)
                rw4 = dict(reads=[r_T], writes=[r_tab])
                P.dve(lambda e, ER=ER: e.tensor_tensor(out=ER, in0=a_[:], in1=b_[:], op=ALU.subtract), **rw4)
                P.dve(lambda e, EI=EI: e.tensor_tensor(out=EI, in0=c_[:], in1=d_[:], op=ALU.add), **rw4)
            P.barrier()
        pxs.close()
        dsk = sb(p4, [128, 8], F32, "dsk"); gam = sb(p4, [128, 8], F32, "gam"); bgl = sb(p4, [128, 8], F32, "bgl"); r_dsk = P.res(); s_dsk = P.new_dma_sem()
        P.dma("sp", dsk[:], I["d_skip"][0].rearrange("(c p) -> p c", p=128), writes=[r_dsk], sem=s_dsk, allow_slow_non_contiguous=True)
        P.dma("sp", gam[:], I["out_norm_ssm"][0].rearrange("(c p) -> p c", p=128), writes=[r_dsk], sem=s_dsk, allow_slow_non_contiguous=True)
        P.dma("sp", bgl[:], I["b_glu"][0].rearrange("(c p) -> p c", p=128), writes=[r_dsk], sem=s_dsk, allow_slow_non_contiguous=True)
        G4 = sb(p4a, [128, 4, 2, L], BF16, "G4"); r_G4 = [P.res() for _ in range(4)]
        hT = sb(p4a, [128, 2, L], BF16, "hT"); r_hT = P.res()
        A1 = sb(p4, [128, 512], F32, "A1"); A2 = sb(p4, [128, 512], F32, "A2"); r_A = P.res()
        TB = [[sb(p4a, [128, 512], F32, "TB") for _ in range(6)] for _ in range(2)]
        r_TB = [[P.res() for _ in range(6)] for _ in range(2)]
        cin4 = sb(p4a, [128, NCH, 4, 2], F32, "cin4"); r_cin = P.res()
        Dt = sb(p4a, [128, 4, 2], F32, "Dt"); T1c = sb(p4a, [128, 4, 2], F32, "T1c"); T2c = sb(p4a, [128, 4, 2], F32, "T2c"); r_ch = P.res()
        Ma = sb(p4a, [128, 4, 2], F32, "Ma"); Mb = sb(p4a, [128, 4, 2], F32, "Mb"); r_M = P.res()
        nLTi = sb(p4a, [128, NP], F32, "nLTi")
        P.dve(lambda e: tsc(e, nLTi[:], LTi[:], -1.0, None, ALU.mult), reads=[r_s], writes=[r_s])
        gT = uT
        NFULL = 2048 // TCH
        tbc = 0
        for k in range(8):
            steps = [(q, bi) for q in range(4) for bi in range(len(BANKS))]
            pend = None
            for (q, bi) in steps:
                pi_ = 4 * k + q
                b0, b1 = BANKS[bi]; N = b1 - b0
                nc_ = max(N // TCH, 1); tw = min(TCH, N)
                tb = tbc % 2; tbc += 1
                T_ = TB[tb]; rT = r_TB[tb]
                for ri in range(2):
                    P.pe(lambda e, ri=ri, pi_=pi_, k=k, b0=b0, b1=b1, N=N: e.matmul(out=pB[ri][:, 0:N], lhsT=Bt[:, pi_, ri, :], rhs=uT[:, k, b0:b1], start=True, stop=True),
                         reads=[r_BC, r_uT[k]], writes=[r_pB[ri]])
                fr = Fre[:, pi_, 0:tw].unsqueeze(1).broadcast_to([128, nc_, tw])
                fi = Fim[:, pi_, 0:tw].unsqueeze(1).broadcast_to([128, nc_, tw])
                v3 = (lambda nc_: (lambda ap: ap.rearrange("p (c t) -> p c t", c=nc_)))(nc_)
                for j, (src, tab) in enumerate(((0, fr), (1, fi), (1, fr), (0, fi))):
                    P.dve(lambda e, j=j, src=src, tab=tab, N=N, v3=v3, T_=T_: e.tensor_tensor(out=v3(T_[j][:, 0:N]), in0=v3(pB[src][:, 0:N]), in1=tab, op=ALU.mult),
                          reads=[r_pB[src], r_tab], writes=[rT[j]])
                P.pool(lambda e, N=N, T_=T_: e.tensor_tensor(out=T_[4][:, 0:N], in0=T_[0][:, 0:N], in1=T_[1][:, 0:N], op=ALU.subtract), reads=[rT[0], rT[1]], writes=[rT[4]])
                P.pool(lambda e, N=N, T_=T_: e.tensor_tensor(out=T_[5][:, 0:N], in0=T_[2][:, 0:N], in1=T_[3][:, 0:N], op=ALU.add), reads=[rT[2], rT[3]], writes=[rT[5]])

                def scans(q=q, b0=b0, b1=b1, N=N, T_=T_, rT=rT):
                    for ri in range(2):
                        P.dve(lambda e, ri=ri: e.tensor_tensor_scan(out=G4[:, q, ri, b0:b1], data0=segm[:, 0:N], data1=T_[4 + ri][:, 0:N], initial=0.0, op0=ALU.mult, op1=ALU.add),
                              reads=[rT[4 + ri], r_io], writes=[r_G4[q]])
                if pend is not None:
                    pend()
                pend = scans
            pend()
            P.dve(lambda e, k=k: e.tensor_copy(out=Ma[:, :, 0], in_=LTr[:, 4 * k:4 * k + 4]), reads=[r_s, r_ch], writes=[r_M])
            P.dve(lambda e, k=k: e.tensor_copy(out=Ma[:, :, 1], in_=LTr[:, 4 * k:4 * k + 4]), reads=[r_s], writes=[r_M])
            P.dve(lambda e, k=k: e.tensor_copy(out=Mb[:, :, 0], in_=LTi[:, 4 * k:4 * k + 4]), reads=[r_s], writes=[r_M])
            P.dve(lambda e, k=k: e.tensor_copy(out=Mb[:, :, 1], in_=nLTi[:, 4 * k:4 * k + 4]), reads=[r_s], writes=[r_M])
            P.dve(lambda e: e.memset(cin4[:, 0, :, :], 0.0), reads=[r_cin], writes=[r_cin])
            for c in range(NCH - 1):
                te = TCH * c + TCH - 1
                P.dve(lambda e, c=c, te=te: e.tensor_tensor(out=Dt[:], in0=G4[:, :, :, te], in1=cin4[:, c, :, :], op=ALU.add), reads=r_G4 + [r_cin, r_ch], writes=[r_ch])
                P.dve(lambda e: e.tensor_tensor(out=T1c[:], in0=Dt[:], in1=Ma[:], op=ALU.mult), reads=[r_ch, r_M], writes=[r_ch])
                P.dve(lambda e: e.tensor_tensor(out=T2c[:], in0=Dt[:], in1=Mb[:], op=ALU.mult), reads=[r_ch, r_M], writes=[r_ch])
                P.dve(lambda e, c=c: e.tensor_tensor(out=cin4[:, c + 1, :, 0], in0=T1c[:, :, 0], in1=T2c[:, :, 1], op=ALU.add), reads=[r_ch], writes=[r_cin])
                P.dve(lambda e, c=c: e.tensor_tensor(out=cin4[:, c + 1, :, 1], in0=T1c[:, :, 1], in1=T2c[:, :, 0], op=ALU.add), reads=[r_ch], writes=[r_cin])
            for (q, bi) in steps:
                pi_ = 4 * k + q
                b0, b1 = BANKS[bi]; N = b1 - b0
                nc_ = max(N // TCH, 1); tw = min(TCH, N)
                c0 = b0 // TCH
                tb = tbc % 2; tbc += 1
                T_ = TB[tb]; rT = r_TB[tb]
                pr = Pre[:, pi_, 0:tw].unsqueeze(1).broadcast_to([128, nc_, tw])
                pim_ = Pim[:, pi_, 0:tw].unsqueeze(1).broadcast_to([128, nc_, tw])
                v3 = (lambda nc_: (lambda ap: ap.rearrange("p (c t) -> p c t", c=nc_)))(nc_)
                cb1 = cin4[:, c0:c0 + nc_, q, 1].unsqueeze(2).broadcast_to([128, nc_, tw])
                P.dve(lambda e, q=q, b0=b0, b1=b1, N=N, v3=v3, T_=T_, cb1=cb1: e.tensor_tensor(out=v3(T_[1][:, 0:N]), in0=v3(G4[:, q, 1, b0:b1]), in1=cb1, op=ALU.add),
                      reads=[r_G4[q], r_cin], writes=[rT[1]])
                for ri in range(1):
                    for cc in range(nc_):
                        P.act(lambda e, ri=ri, q=q, b0=b0, cc=cc, tw=tw, c0=c0, T_=T_: e.activation(out=T_[ri][:, cc * tw:(cc + 1) * tw], in_=G4[:, q, ri, b0 + cc * tw:b0 + (cc + 1) * tw],
                                                                                          func=AF.Identity, bias=cin4[:, c0 + cc, q, ri:ri + 1], scale=1.0),
                              reads=[r_G4[q], r_cin], writes=[rT[ri]])
                for j, (src, tab) in enumerate(((0, pr), (1, pim_), (0, pim_), (1, pr))):
                    P.dve(lambda e, j=j, src=src, tab=tab, N=N, v3=v3, T_=T_: e.tensor_tensor(out=v3(T_[2 + j][:, 0:N]), in0=v3(T_[src][:, 0:N]), in1=tab, op=ALU.mult),
                          reads=[rT[src], r_tab], writes=[rT[2 + j]])
                P.pool(lambda e, b0=b0, b1=b1, N=N, T_=T_: e.tensor_tensor(out=hT[:, 0, b0:b1], in0=T_[2][:, 0:N], in1=T_[3][:, 0:N], op=ALU.subtract), reads=[rT[2], rT[3]], writes=[r_hT])
                P.pool(lambda e, b0=b0, b1=b1, N=N, T_=T_: e.tensor_tensor(out=hT[:, 1, b0:b1], in0=T_[4][:, 0:N], in1=T_[5][:, 0:N], op=ALU.add), reads=[rT[4], rT[5]], writes=[r_hT])
                for ri in range(2):
                    P.pe(lambda e, ri=ri, bi=bi, pi_=pi_, q=q, b0=b0, b1=b1, N=N: e.matmul(out=pY[bi][:, 0:N], lhsT=Ct[:, pi_, ri, :], rhs=hT[:, ri, b0:b1],
                                                                                 start=(q == 0 and ri == 0), stop=(q == 3 and ri == 1)),
                         reads=[r_BC, r_hT], writes=[r_pY[bi]])
            for bi, (b0, b1) in enumerate(BANKS):
                N = b1 - b0
                P.dve(lambda e, bi=bi, k=k, b0=b0, b1=b1, N=N: e.scalar_tensor_tensor(out=A1[:, 0:N], in0=uT[:, k, b0:b1], scalar=dsk[:, k:k + 1], in1=pY[bi][:, 0:N], op0=ALU.mult, op1=ALU.add),
                      reads=[r_uT[k], r_dsk, r_pY[bi]], writes=[r_A])
                P.dve(lambda e, N=N: e.tensor_tensor(out=A2[:, 0:N], in0=A1[:, 0:N], in1=A1[:, 0:N], op=ALU.mult), reads=[r_A], writes=[r_A])
                P.dve(lambda e, N=N: tsc(e, A2[:, 0:N], A2[:, 0:N], 0.044715, 1.0, ALU.mult, ALU.add), reads=[r_A], writes=[r_A])
                P.dve(lambda e, N=N: e.tensor_tensor(out=A2[:, 0:N], in0=A2[:, 0:N], in1=A1[:, 0:N], op=ALU.mult), reads=[r_A], writes=[r_A])
                P.act(lambda e, N=N: e.activation(out=A2[:, 0:N], in_=A2[:, 0:N], func=AF.Sigmoid, scale=1.5957691216057308), reads=[r_A], writes=[r_A])
                P.dve(lambda e, k=k, b0=b0, b1=b1, N=N: e.tensor_tensor(out=gT[:, k, b0:b1], in0=A1[:, 0:N], in1=A2[:, 0:N], op=ALU.mult), reads=[r_A], writes=[r_uT[k]])
        P.barrier()
        p4a.close()
        yagT = sb(mix, [128, 8, L], BF16, "yagT"); r_yag = P.res()
        ssa = sb(mix, [128, NT, 8], F32, "ssa"); r_ssa = P.res()
        P.pool(lambda e: e.memset(ssa[:], 1.0), writes=[r_ssa])
        wglu = sb(p4, [128, 8, DSSM], BF16, "wglu"); r_wglu = P.res(); s_wglu = P.new_dma_sem()
        P.dma("sp", wglu[:], wglu_bf, reads=[r_cv2], writes=[r_wglu], sem=s_wglu)
        GA1 = [sb(p4, [128, 512], F32, "GA1") for _ in range(2)]; GA2 = [sb(p4, [128, 512], F32, "GA2") for _ in range(2)]
        GX = [sb(p4, [128, 512], BF16, "GX") for _ in range(2)]
        r_GA = [P.res() for _ in range(2)]; r_GX = [P.res() for _ in range(2)]
        gstep = 0
        pend_g = None
        for fo in range(8):
            for bi, (b0, b1) in enumerate(BANKS):
                N = b1 - b0
                z = gstep % 2; gstep += 1
                for kc in range(8):
                    P.pe(lambda e, z=z, kc=kc, fo=fo, b0=b0, b1=b1, N=N: e.matmul(out=pB[z][:, 0:N], lhsT=wglu[:, kc, fo * 128:(fo + 1) * 128], rhs=gT[:, kc, b0:b1], start=(kc == 0), stop=(kc == 7)),
                         reads=[r_wglu] + r_uT, writes=[r_pB[z]])
                P.act(lambda e, z=z, fo=fo, N=N: e.activation(out=GA1[z][:, 0:N], in_=pB[z][:, 0:N], func=AF.Sigmoid, bias=bgl[:, fo:fo + 1], scale=1.0), reads=[r_pB[z], r_dsk], writes=[r_GA[z]])
                P.dve(lambda e, z=z, fo=fo, b0=b0, b1=b1, N=N: e.tensor_tensor(out=GA2[z][:, 0:N], in0=GA1[z][:, 0:N], in1=gT[:, fo, b0:b1], op=ALU.mult), reads=[r_GA[z], r_uT[fo]], writes=[r_GA[z]])
                P.dve(lambda e, z=z, fo=fo, b0=b0, b1=b1, N=N: tsc(e, yagT[:, fo, b0:b1], GA2[z][:, 0:N], gam[:, fo:fo + 1], None, ALU.mult), reads=[r_GA[z], r_dsk], writes=[r_yag])
                P.act(lambda e, z=z, N=N: e.activation(out=GX[z][:, 0:N], in_=GA2[z][:, 0:N], func=AF.Square), reads=[r_GA[z]], writes=[r_GX[z]])

                def gstats(z=z, fo=fo, b0=b0, b1=b1):
                    tl = [i for i in range(NT) if b0 <= 128 * i < b1]
                    n = TS[tl[0]]
                    for i in tl:
                        P.pe(lambda e, i=i, n=n: e.matmul(out=pst[0:n, i:i + 1], lhsT=GX[z][:, 128 * i - b0:128 * i - b0 + n], rhs=ones_bf[:, 0:1], start=True, stop=True),
                             reads=[r_GX[z], r_const], writes=[r_pst])
                    P.dve(lambda e, i0=tl[0], i1=tl[-1] + 1, n=n: e.tensor_copy(out=ssa[0:n, i0:i1, fo], in_=pst[0:n, i0:i1]), reads=[r_pst], writes=[r_ssa])
                if pend_g is not None:
                    pend_g()
                pend_g = gstats
        pend_g()
        P.barrier()

    if stage == 4:
        dump(yagT[:, 0, 0:512], r_yag, 512)
        dump(yagT[:, 7, 1552:2064], r_yag, 512)
        dump(yagT[:, 3, 512:1024], r_yag, 512)
        dump(ssa[:].rearrange("p a b -> p (a b)"), r_ssa, NT * 8)
        P.stopped = True

    rsa = sb(mix, [128, NT], F32, "rsa"); rsb = sb(mix, [128, NT], F32, "rsb"); r_rs = P.res()
    P.dve(lambda e: e.tensor_reduce(out=rsa[:], in_=ssa[:], axis=AX.X, op=ALU.add), reads=[r_ssa], writes=[r_rs])
    P.dve(lambda e: e.tensor_reduce(out=rsb[:], in_=ssb[:], axis=AX.X, op=ALU.add), reads=[r_ssb], writes=[r_rs])
    rstd_ops(rsa[:], r_rs, 1024)
    rstd_ops(rsb[:], r_rs, 1024)
    s_h1 = P.new_dma_sem()
    r_h1s = P.res("h1s")
    with contextlib.ExitStack() as p5:
        wo = [sb(p5, [128, 16, 512], BF16, "wo") for _ in range(2)]; r_wo = [P.res() for _ in range(2)]; s_wo = [P.new_dma_sem() for _ in range(2)]
        wo_v = I["w_out"][0].rearrange("(kc p) f -> p kc f", p=128)
        xr = [sb(p5, [128, 512], F32, "xr") for _ in range(2)]; r_xr = [P.res() for _ in range(2)]; s_xr = [P.new_dma_sem() for _ in range(2)]
        ho = [sb(p5, [128, 512], F32, "ho") for _ in range(2)]; r_ho = [P.res() for _ in range(2)]; s_ho = [P.new_dma_sem() for _ in range(2)]
        pA_ = [ps(p5, [128, 512], F32, "pA") for _ in range(2)]; r_pA = [P.pres() for _ in range(2)]
        pB_ = [ps(p5, [128, 512], F32, "pBo") for _ in range(2)]; r_pBo = [P.pres() for _ in range(2)]
        cnt = 0
        for cb in range(4):
            w = cb % 2
            P.dma("sp", wo[w][:], wo_bf[cb], reads=[r_cv2], writes=[r_wo[w]], sem=s_wo[w])
            for i in range(NT):
                n = TS[i]; s = cnt % 2; cnt += 1
                load_h_tile(xr[s], i, cb * 512, (cb + 1) * 512, r_xr[s], s_xr[s])
                for kc in range(8):
                    P.pe(lambda e, s=s, w=w, kc=kc, i=i, n=n: e.matmul(out=pA_[s][0:n, :], lhsT=yagT[:, kc, 128 * i:128 * i + n], rhs=wo[w][:, kc, :], start=(kc == 0), stop=(kc == 7)),
                         reads=[r_yag, r_wo[w]], writes=[r_pA[s]])
                for kc in range(8):
                    P.pe(lambda e, s=s, w=w, kc=kc, i=i, n=n: e.matmul(out=pB_[s][0:n, :], lhsT=ybgT[:, kc, 128 * i:128 * i + n], rhs=wo[w][:, 8 + kc, :], start=(kc == 0), stop=(kc == 7)),
                         reads=[r_ybg, r_wo[w]], writes=[r_pBo[s]])
                P.dve(lambda e, s=s, i=i, n=n: e.scalar_tensor_tensor(out=ho[s][0:n, :], in0=pA_[s][0:n, :], scalar=rsa[0:n, i:i + 1], in1=xr[s][0:n, :], op0=ALU.mult, op1=ALU.add),
                      reads=[r_pA[s], r_rs, r_xr[s]], writes=[r_ho[s]])
                P.dve(lambda e, s=s, i=i, n=n: e.scalar_tensor_tensor(out=ho[s][0:n, :], in0=pB_[s][0:n, :], scalar=rsb[0:n, i:i + 1], in1=ho[s][0:n, :], op0=ALU.mult, op1=ALU.add),
                      reads=[r_pBo[s], r_rs, r_ho[s]], writes=[r_ho[s]])
                P.dma("sp", h1s[128 * i:128 * i + n, cb * 512:(cb + 1) * 512], ho[s][0:n, :], reads=[r_ho[s]], writes=[r_h1s], sem=s_ho[s])
        P.barrier()
    mix.close()

    with contextlib.ExitStack() as p6:
        gffn = sb(p6, [128, D], F32, "gffn"); gfin = sb(p6, [128, D], F32, "gfin"); r_gf = P.res(); s_gf = P.new_dma_sem()
        P.dma("sp", gffn[:], I["ffn_norm"][0:1, :].broadcast_to([128, D]), writes=[r_gf], sem=s_gf)
        P.dma("sp", gfin[:], I["final_norm"].unsqueeze(0).broadcast_to([128, D]), writes=[r_gf], sem=s_gf)
        cw = sb(p6, [128, NFC, 3], F32, "cw"); cbias = sb(p6, [128, NFC], F32, "cbias")
        for kk in range(3):
            P.dma("sp", cw[:, :, kk], I["conv_w"][0][kk].rearrange("(fc p) -> p fc", p=128), writes=[r_gf], sem=s_gf, allow_slow_non_contiguous=True)
        P.dma("sp", cbias[:], I["conv_b"][0].rearrange("(fc p) -> p fc", p=128), writes=[r_gf], sem=s_gf, allow_slow_non_contiguous=True)
        halo = sb(p6, [128, NFC, 2], F32, "halo"); r_halo = P.res()
        P.pool(lambda e: e.memset(halo[:], 0.0), writes=[r_halo])
        h1g = sb(p6, [128, 5, D], F32, "h1g"); r_h1g = [P.res() for _ in range(5)]; s_h1g = [P.new_dma_sem() for _ in range(5)]
        xn2T = sb(p6, [128, 16, 528], BF16, "xn2T"); r_xn2T = P.res()
        actT = sb(p6, [128, NFC, 528], BF16, "actT"); r_actT = P.res()
        wup = [sb(p6, [128, 16, 256], BF16, "wup") for _ in range(2)]; r_wup = [P.res() for _ in range(2)]; s_wup = [P.new_dma_sem() for _ in range(2)]
        wdn = [sb(p6, [128, NFC, 128], BF16, "wdn") for _ in range(2)]; r_wdn = [P.res() for _ in range(2)]; s_wdn = [P.new_dma_sem() for _ in range(2)]
        wup_v = I["w_up"][0].rearrange("(kc p) f -> p kc f", p=128)
        wdn_v = I["w_down"][0].rearrange("(fc p) o -> p fc o", p=128)
        junk2 = sb(p6, [128, D], BF16, "junk2"); r_j2 = P.res()
        xnb2_ = [sb(p6, [128, D], BF16, "xnb2") for _ in range(2)]; r_xnb2_ = [P.res() for _ in range(2)]
        ss2 = sb(p6, [128, 16], F32, "ss2"); r_ss2_ = [P.res() for _ in range(16)]
        gbuf = sb(p6, [128, 516], F32, "gbuf"); r_gbuf = P.res()
        cbuf = sb(p6, [128, 512], F32, "cbuf"); r_cbuf = P.res()
        ffo = [sb(p6, [128, 512], F32, "ffo") for _ in range(2)]; r_ffo = [P.res() for _ in range(2)]
        ot = [sb(p6, [128, D], F32, "ot") for _ in range(1)]; r_ot = [P.res() for _ in range(1)]; s_ot = [P.new_dma_sem() for _ in range(1)]
        pT2 = [ps(p6, [128, 8, 128], BF16, "pT2") for _ in range(2)]; r_pT2 = [P.pres() for _ in range(2)]
        pG = [ps(p6, [128, 512], F32, "pG") for _ in range(2)]; r_pG = [P.pres() for _ in range(2)]
        pV = [ps(p6, [128, 512], F32, "pV") for _ in range(2)]; r_pV = [P.pres() for _ in range(2)]
        pTr = [ps(p6, [128, 512], F32, "pTr") for _ in range(2)]; r_pTr = [P.pres() for _ in range(2)]
        tc2 = 0; uc = 0; dc = 0; oc_ = 0; trcl = [0]
        from collections import deque
        ha = sb(p6, [128, D], F32, "ha"); r_ha = P.res(); s_ha = P.new_dma_sem()
        tc2l = [0]

        def stage_a_pieces(gi):
            ta, tb_ = GROUPS[gi]
            pcs = []
            for ti, i in enumerate(range(ta, tb_)):
                def pc(ti=ti, i=i, n=TS[i], gi=gi):
                    sc = 8 * (gi % 2) + ti if False else ti
                    P.dma("sp", ha[0:n, :], h1s[128 * i:128 * i + n, :], reads=[r_h1s], writes=[r_ha], sem=s_ha)
                    P.act(lambda e: e.activation(out=junk2[0:n, :], in_=ha[0:n, :], func=AF.Square, accum_out=ss2[0:n, ti:ti + 1]), reads=[r_ha], writes=[r_j2, r_ss2_[ti]])
                    rstd_ops(ss2[0:n, ti:ti + 1], r_ss2_[ti], D)
                    P.dve(lambda e: e.scalar_tensor_tensor(out=xnb2_[ti % 2][0:n, :], in0=ha[0:n, :], scalar=ss2[0:n, ti:ti + 1], in1=gffn[0:n, :], op0=ALU.mult, op1=ALU.mult),
                          reads=[r_ha, r_ss2_[ti], r_gf], writes=[r_xnb2_[ti % 2]])
                    for k4 in range(4):
                        pp = tc2l[0] % 2; tc2l[0] += 1
                        for j in range(4):
                            kc = k4 * 4 + j
                            P.pe(lambda e, pp=pp, j=j, kc=kc: e.transpose(out=pT2[pp][:, j, 0:n], in_=xnb2_[ti % 2][0:n, kc * 128:(kc + 1) * 128], identity=ident_bf[0:n, 0:n]),
                                 reads=[r_xnb2_[ti % 2], r_const], writes=[r_pT2[pp]])
                        P.act(lambda e, pp=pp, k4=k4: e.activation(out=xn2T[:, k4 * 4:k4 * 4 + 4, 128 * ti:128 * ti + n], in_=pT2[pp][:, 0:4, 0:n], func=AF.Copy),
                              reads=[r_pT2[pp]], writes=[r_xn2T])
                pcs.append(pc)
            return pcs

        def load_h1g(gi):
            ta, tb_ = GROUPS[gi]
            for ti, i in enumerate(range(ta, tb_)):
                n = TS[i]
                P.dma("sp", h1g[0:n, ti, :], h1s[128 * i:128 * i + n, :], reads=[r_h1s], writes=[r_h1g[ti]], sem=s_h1g[ti])

        for gidx, (t0, t1_) in enumerate(GROUPS):
            g0 = 128 * t0; g1 = min(128 * t1_, L); GW = g1 - g0
            gbanks = [(0, min(GW, 512))] + ([(512, GW)] if GW > 512 else [])
            if gidx == 0:
                for pc_ in stage_a_pieces(0):
                    pc_()
                load_h1g(0)
            nxt_a = deque(stage_a_pieces(gidx + 1)) if gidx + 1 < len(GROUPS) else deque()
            for fc in range(NFC):
                w = uc % 2; uc += 1
                P.dma("sp", wup[w][:], wup_bf[fc], reads=[r_cv], writes=[r_wup[w]], sem=s_wup[w])
                for (c0, c1) in gbanks:
                    N = c1 - c0
                    z = dc % 2; dc += 1
                    for kc in range(16):
                        P.pe(lambda e, z=z, w=w, kc=kc, c0=c0, c1=c1, N=N: e.matmul(out=pG[z][:, 0:N], lhsT=wup[w][:, kc, 0:128], rhs=xn2T[:, kc, c0:c1], start=(kc == 0), stop=(kc == 15)),
                             reads=[r_wup[w], r_xn2T], writes=[r_pG[z]])
                    for kc in range(16):
                        P.pe(lambda e, z=z, w=w, kc=kc, c0=c0, c1=c1, N=N: e.matmul(out=pV[z][:, 0:N], lhsT=wup[w][:, kc, 128:256], rhs=xn2T[:, kc, c0:c1], start=(kc == 0), stop=(kc == 15)),
                             reads=[r_wup[w], r_xn2T], writes=[r_pV[z]])
                    P.act(lambda e, fc=fc: e.activation(out=gbuf[:, 0:2], in_=halo[:, fc, :], func=AF.Copy), reads=[r_halo], writes=[r_gbuf])
                    P.act(lambda e, z=z, N=N: e.activation(out=gbuf[:, 2:2 + N], in_=pG[z][:, 0:N], func=AF.Copy), reads=[r_pG[z]], writes=[r_gbuf])
                    P.act(lambda e, fc=fc, N=N: e.activation(out=halo[:, fc, :], in_=gbuf[:, N:N + 2], func=AF.Copy), reads=[r_gbuf], writes=[r_halo])
                    P.dve(lambda e, fc=fc, N=N: tsc(e, cbuf[:, 0:N], gbuf[:, 2:2 + N], cw[:, fc, 2:3], cbias[:, fc:fc + 1], ALU.mult, ALU.add), reads=[r_gbuf, r_gf], writes=[r_cbuf])
                    P.dve(lambda e, fc=fc, N=N: e.scalar_tensor_tensor(out=cbuf[:, 0:N], in0=gbuf[:, 1:1 + N], scalar=cw[:, fc, 1:2], in1=cbuf[:, 0:N], op0=ALU.mult, op1=ALU.add), reads=[r_gbuf, r_gf, r_cbuf], writes=[r_cbuf])
                    P.dve(lambda e, fc=fc, N=N: e.scalar_tensor_tensor(out=cbuf[:, 0:N], in0=gbuf[:, 0:N], scalar=cw[:, fc, 0:1], in1=cbuf[:, 0:N], op0=ALU.mult, op1=ALU.add), reads=[r_gbuf, r_gf, r_cbuf], writes=[r_cbuf])
                    P.act(lambda e, N=N: e.activation(out=cbuf[:, 0:N], in_=cbuf[:, 0:N], func=AF.Silu), reads=[r_cbuf], writes=[r_cbuf])
                    P.dve(lambda e, z=z, fc=fc, c0=c0, c1=c1, N=N: e.tensor_tensor(out=actT[:, fc, c0:c1], in0=cbuf[:, 0:N], in1=pV[z][:, 0:N], op=ALU.mult), reads=[r_cbuf, r_pV[z]], writes=[r_actT])
            pend_c = None
            for oc in range(16):
                w = oc_ % 2; oc_ += 1
                P.dma("sp", wdn[w][:], wdn_bf[oc], reads=[r_cv], writes=[r_wdn[w]], sem=s_wdn[w])
                for (c0, c1) in gbanks:
                    N = c1 - c0
                    z = dc % 2; dc += 1
                    for fc in range(NFC):
                        P.pe(lambda e, z=z, w=w, fc=fc, c0=c0, c1=c1, N=N: e.matmul(out=pG[z][:, 0:N], lhsT=wdn[w][:, fc, :], rhs=actT[:, fc, c0:c1], start=(fc == 0), stop=(fc == NFC - 1)),
                             reads=[r_wdn[w], r_actT], writes=[r_pG[z]])
                    P.act(lambda e, z=z, N=N: e.activation(out=ffo[z][:, 0:N], in_=pG[z][:, 0:N], func=AF.Copy), reads=[r_pG[z]], writes=[r_ffo[z]])

                    def trs(z=z, c0=c0, c1=c1, oc=oc, t0=t0, t1_=t1_):
                        nonlocal_trc = trcl
                        for ti, i in enumerate(range(t0, t1_)):
                            if not (c0 <= 128 * ti < c1):
                                continue
                            n = TS[i]; q = nonlocal_trc[0] % 2; nonlocal_trc[0] += 1
                            lo = 128 * ti - c0
                            P.pe(lambda e, q=q, z=z, lo=lo, n=n: e.transpose(out=pTr[q][0:n, 0:128], in_=ffo[z][:, lo:lo + n], identity=ident_f[:]),
                                 reads=[r_ffo[z], r_const], writes=[r_pTr[q]])
                            P.dve(lambda e, q=q, ti=ti, n=n, oc=oc: e.tensor_tensor(out=h1g[0:n, ti, oc * 128:(oc + 1) * 128], in0=h1g[0:n, ti, oc * 128:(oc + 1) * 128], in1=pTr[q][0:n, 0:128], op=ALU.add),
                                  reads=[r_pTr[q], r_h1g[ti]], writes=[r_h1g[ti]])
                    if pend_c is not None:
                        pend_c()
                    pend_c = trs
                if oc % 3 == 2 and nxt_a:
                    nxt_a.popleft()()
            pend_c()
            while nxt_a:
                nxt_a.popleft()()
            for ti, i in enumerate(range(t0, t1_)):
                n = TS[i]; o = 0
                P.act(lambda e, ti=ti, n=n: e.activation(out=junk2[0:n, :], in_=h1g[0:n, ti, :], func=AF.Square, accum_out=ss2[0:n, 8 + ti:9 + ti]), reads=[r_h1g[ti]], writes=[r_j2, r_ss2_[8 + ti]])
                rstd_ops(ss2[0:n, 8 + ti:9 + ti], r_ss2_[8 + ti], D)
                P.dve(lambda e, ti=ti, n=n, o=o: e.scalar_tensor_tensor(out=ot[o][0:n, :], in0=h1g[0:n, ti, :], scalar=ss2[0:n, 8 + ti:9 + ti], in1=gfin[0:n, :], op0=ALU.mult, op1=ALU.mult),
                      reads=[r_h1g[ti], r_ss2_[8 + ti], r_gf], writes=[r_ot[o]])
                if i == 0:
                    P.dma("sp", out[0:112, :], ot[o][16:128, :], reads=[r_ot[o]], sem=s_ot[o])
                else:
                    P.dma("sp", out[128 * i - 16:128 * i - 16 + n, :], ot[o][0:n, :], reads=[r_ot[o]], sem=s_ot[o])
            if gidx + 1 < len(GROUPS):
                load_h1g(gidx + 1)
        P.barrier()
    P.stopped = False
    P.barrier(final=True)
    P.emit(top)
    top.close()
    return nc


_NC_CACHE = {}


def kernel(**inputs):
    consts = host_consts()
    if "nc" not in _NC_CACHE:
        _NC_CACHE["nc"] = build_nc()
    nc = _NC_CACHE["nc"]
    x = np.asarray(inputs["x"], dtype=np.float32)
    B = x.shape[0]
    in_maps = []
    shared = {}
    for name, shape in IN_SPECS:
        if name == "x":
            continue
        shared[name] = np.ascontiguousarray(np.asarray(inputs[name], dtype=np.float32).reshape(shape))
    for name, shape, dt in CONST_SPECS:
        shared["c_" + name] = consts[name]
    for b in range(B):
        m = dict(shared)
        m["x"] = np.ascontiguousarray(x[b])
        in_maps.append(m)
    res = run_bass_kernel_spmd(nc, in_maps, core_ids=list(range(B)))
    return np.stack([np.asarray(r["out"], dtype=np.float32) for r in res.results], axis=0)
```
